# Optimizing a Trainium2 kernel written in Bass

```python
import math
import jax, jax.numpy as jnp
from jax import lax
import numpy as np

D_MODEL = 2048
BATCH = 1
SEQ = 16384
DEPTH = 4
DEC_BATCH = 2
DEC_SEQ = 4096
PAST_LEN = 128

HEAD_DIM = 128
N_HEADS_A = 8
N_HEADS_B = 8
D_A = N_HEADS_A * HEAD_DIM
D_B = N_HEADS_B * HEAD_DIM
D_MIX = D_A + D_B
QK_SCALE = HEAD_DIM ** -0.5
DIL_PATTERNS = ((128, 1), (512, 4), (2048, 16))
N_BUCKETS = 32
MAX_DISTANCE = 1024
GRID_W = 64
NA_ROWS = 8
NA_COLS = 16
D_FF = -(-8 * D_MODEL // (3 * 256)) * 256
EPS = 1e-6

kernel_name = "hymba_dilated_natten_encoder"


def rms_norm(x, g):
    xf = x.astype(jnp.float32)
    y = xf * lax.rsqrt(jnp.mean(xf * xf, axis=-1, keepdims=True) + EPS)
    return (y * g.astype(jnp.float32)).astype(x.dtype)


def t5_bucket(rel):
    nb = N_BUCKETS // 2
    exact = nb // 2
    n = np.abs(rel)
    sign = np.where(rel > 0, nb, 0)
    large = exact + (np.log(np.maximum(n, 1) / exact) / math.log(MAX_DISTANCE / exact) * (nb - exact)).astype(np.int64)
    large = np.minimum(large, nb - 1)
    return (sign + np.where(n < exact, n, large)).astype(np.int32)


def banded_attention_stats(q, k, v, bias, radius):
    blk = radius
    B, H, G, L, dh = q.shape
    nblk = -(-L // blk)
    Lp = nblk * blk
    pad = Lp - L
    qb = jnp.pad(q, ((0, 0),) * 3 + ((0, pad), (0, 0))).reshape(B, H, G, nblk, blk, dh)
    padk = ((0, 0),) * 3 + ((blk, pad + blk), (0, 0))
    kp = jnp.pad(k, padk)
    vp = jnp.pad(v, padk)

    def windows(a):
        return jnp.concatenate([a[..., j * blk:j * blk + Lp, :].reshape(B, H, G, nblk, blk, dh) for j in range(3)], axis=-2)

    kb, vb = windows(kp), windows(vp)
    i = np.arange(blk)[:, None]
    j = np.arange(3 * blk)[None, :]
    pos = np.arange(nblk)[:, None, None] * blk - blk + j[None]
    valid = (pos >= 0) & (pos < L) & (np.abs(j - blk - i) <= radius)[None]
    logits = jnp.einsum('bhgnqd,bhgnkd->bhgnqk', qb, kb, preferred_element_type=jnp.float32)
    logits = jnp.where(valid, logits + bias[:, None, None].astype(jnp.float32), -jnp.inf)
    m = jnp.max(logits, axis=-1)
    p = jnp.exp(logits - m[..., None])
    s = jnp.sum(p, axis=-1)
    num = jnp.einsum('bhgnqk,bhgnkd->bhgnqd', p, vb.astype(jnp.float32))
    return (num.reshape(B, H, G, Lp, dh)[..., :L, :],
            m.reshape(B, H, G, Lp)[..., :L],
            s.reshape(B, H, G, Lp)[..., :L])


def dilated_mixture_attention(q, k, v, t5_table):
    B, H, T, dh = q.shape
    nums, ms, ss = [], [], []
    for window, dil in DIL_PATTERNS:
        radius = window // (2 * dil)
        L = T // dil

        def to_sub(a):
            return a.reshape(B, H, L, dil, dh).swapaxes(2, 3)

        rel = (np.arange(3 * radius)[None, :] - radius - np.arange(radius)[:, None]) * dil
        bias = jnp.transpose(t5_table[t5_bucket(rel)], (2, 0, 1))
        num, m, s = banded_attention_stats(to_sub(q), to_sub(k), to_sub(v), bias, radius)
        nums.append(num.swapaxes(2, 3).reshape(B, H, T, dh))
        ms.append(m.swapaxes(2, 3).reshape(B, H, T))
        ss.append(s.swapaxes(2, 3).reshape(B, H, T))
    m_all = jnp.stack(ms)
    w = jnp.exp(m_all - jnp.max(m_all, axis=0))
    out = jnp.sum(w[..., None] * jnp.stack(nums), axis=0) / jnp.sum(w * jnp.stack(ss), axis=0)[..., None]
    return out.astype(v.dtype)


def neighborhood_attention(q, k, v, rpb):
    B, H, T, dh = q.shape
    rows = T // GRID_W
    kh = min(NA_ROWS, rows)
    r = np.arange(rows)
    rstart = np.clip(r - kh // 2, 0, rows - kh)
    ridx = rstart[:, None] + np.arange(kh)[None, :]
    c = np.arange(GRID_W)
    cstart = np.clip(c - NA_COLS // 2, 0, GRID_W - NA_COLS)
    cmask = (c[None, :] >= cstart[:, None]) & (c[None, :] < cstart[:, None] + NA_COLS)
    mask = np.broadcast_to(cmask[:, None, :], (GRID_W, kh, GRID_W)).reshape(GRID_W, kh * GRID_W)
    roff = ridx - r[:, None] + NA_ROWS - 1
    coff = np.clip(c[None, :] - c[:, None], -(NA_COLS - 1), NA_COLS - 1) + NA_COLS - 1
    bias = rpb[:, roff[:, None, :, None], coff[None, :, None, :]]
    bias = bias.reshape(H, rows, GRID_W, kh * GRID_W)
    qg = q.reshape(B, H, rows, GRID_W, dh)
    kg = k.reshape(B, H, rows, GRID_W, dh)[:, :, ridx].reshape(B, H, rows, kh * GRID_W, dh)
    vg = v.reshape(B, H, rows, GRID_W, dh)[:, :, ridx].reshape(B, H, rows, kh * GRID_W, dh)
    logits = jnp.einsum('bhrqd,bhrkd->bhrqk', qg, kg, preferred_element_type=jnp.float32)
    logits = jnp.where(mask, logits + bias[None].astype(jnp.float32), -jnp.inf)
    p = jax.nn.softmax(logits, axis=-1)
    out = jnp.einsum('bhrqk,bhrkd->bhrqd', p, vg.astype(jnp.float32))
    return out.reshape(B, H, T, dh).astype(v.dtype)


def trunk(x, c, t5_table, norm1_g, norm2_g, w_ada, b_ada, w_in, out_norm_a, out_norm_b,
          w_out, na_rpb, w_gate, w_up, w_down, final_g):
    B, T, _ = x.shape

    def heads(a, n):
        return a.reshape(B, T, n, HEAD_DIM).transpose(0, 2, 1, 3)

    def merge(a):
        return a.transpose(0, 2, 1, 3).reshape(B, T, -1)

    for l in range(DEPTH):
        mod = (jax.nn.silu(c) @ w_ada[l] + b_ada[l])[:, None, :]
        sh1, sc1, g1, sh2, sc2, g2 = jnp.split(mod, 6, axis=-1)
        h = rms_norm(x, norm1_g[l]) * (1.0 + sc1) + sh1
        qkv = h @ w_in[l]
        qa, ka, va, qb, kb, vb = jnp.split(
            qkv, [D_A, 2 * D_A, 3 * D_A, 3 * D_A + D_B, 3 * D_A + 2 * D_B], axis=-1)
        ya = dilated_mixture_attention(heads(qa, N_HEADS_A) * QK_SCALE, heads(ka, N_HEADS_A),
                                       heads(va, N_HEADS_A), t5_table)
        yb = neighborhood_attention(heads(qb, N_HEADS_B) * QK_SCALE, heads(kb, N_HEADS_B),
                                    heads(vb, N_HEADS_B), na_rpb[l])
        y = jnp.concatenate([rms_norm(merge(ya), out_norm_a[l]),
                             rms_norm(merge(yb), out_norm_b[l])], axis=-1) @ w_out[l]
        x = x + g1 * y
        h = rms_norm(x, norm2_g[l]) * (1.0 + sc2) + sh2
        x = x + g2 * ((jax.nn.silu(h @ w_gate[l]) * (h @ w_up[l])) @ w_down[l])
    return rms_norm(x, final_g)


def setup_inputs(seed: int = 0) -> dict:
    key = jax.random.key(seed)
    ks = jax.random.split(key, 18)
    f32 = jnp.float32

    def nrm(k, shape, scale):
        return jax.random.normal(k, shape, f32) * scale

    return {
        'x_prompt': nrm(ks[0], (BATCH, SEQ, D_MODEL), 1.0),
        'x_sample': nrm(ks[1], (DEC_BATCH, DEC_SEQ, D_MODEL), 1.0),
        'c_prompt': nrm(ks[2], (BATCH, D_MODEL), 1.0),
        'c_sample': nrm(ks[3], (DEC_BATCH, D_MODEL), 1.0),
        't5_table': nrm(ks[4], (N_BUCKETS, N_HEADS_A), 0.1),
        'norm1_g': 1.0 + nrm(ks[5], (DEPTH, D_MODEL), 0.02),
        'norm2_g': 1.0 + nrm(ks[6], (DEPTH, D_MODEL), 0.02),
        'w_ada': nrm(ks[7], (DEPTH, D_MODEL, 6 * D_MODEL), 0.5 * D_MODEL ** -0.5),
        'b_ada': nrm(ks[8], (DEPTH, 6 * D_MODEL), 0.02),
        'w_in': nrm(ks[9], (DEPTH, D_MODEL, 3 * D_MIX), D_MODEL ** -0.5),
        'out_norm_a': 1.0 + nrm(ks[10], (DEPTH, D_A), 0.02),
        'out_norm_b': 1.0 + nrm(ks[11], (DEPTH, D_B), 0.02),
        'w_out': nrm(ks[12], (DEPTH, D_MIX, D_MODEL), D_MIX ** -0.5),
        'na_rpb': nrm(ks[13], (DEPTH, N_HEADS_B, 2 * NA_ROWS - 1, 2 * NA_COLS - 1), 0.1),
        'w_gate': nrm(ks[14], (DEPTH, D_MODEL, D_FF), D_MODEL ** -0.5),
        'w_up': nrm(ks[15], (DEPTH, D_MODEL, D_FF), D_MODEL ** -0.5),
        'w_down': nrm(ks[16], (DEPTH, D_FF, D_MODEL), D_FF ** -0.5),
        'final_g': 1.0 + nrm(ks[17], (D_MODEL,), 0.02),
    }


def reference(x_prompt, x_sample, c_prompt, c_sample, t5_table, norm1_g, norm2_g, w_ada, b_ada,
              w_in, out_norm_a, out_norm_b, w_out, na_rpb, w_gate, w_up, w_down, final_g):
    y_prompt = trunk(x_prompt, c_prompt, t5_table, norm1_g, norm2_g, w_ada, b_ada, w_in,
                     out_norm_a, out_norm_b, w_out, na_rpb, w_gate, w_up, w_down, final_g)
    y_sample = trunk(x_sample, c_sample, t5_table, norm1_g, norm2_g, w_ada, b_ada, w_in,
                     out_norm_a, out_norm_b, w_out, na_rpb, w_gate, w_up, w_down, final_g)
    return (y_prompt, y_sample)
```

```python
import math
import contextlib
import numpy as np
import concourse.bass as bass
import concourse.mybir as mybir
from concourse.bass_utils import run_bass_kernel_spmd

F32 = mybir.dt.float32
BF16 = mybir.dt.bfloat16
AF = mybir.ActivationFunctionType
ALU = mybir.AluOpType

D = 2048
KC = 16
DFF = 5632
FC = 44
FH = 22
NH = 16
HD = 128
QK_SCALE = HD ** -0.5
EPS = 1e-6
HT = 8
BT = 8
GA_W = 2304
GB_W = 1920
N_BUCKETS = 32
MAX_DISTANCE = 1024

ENGINES = ["pe", "act", "dve", "pool", "sp"]
NDSEM = 8


class Buf:
    __slots__ = ("name", "w", "r")

    def __init__(self, name):
        self.name = name
        self.w = None
        self.r = []


class Prog:
    def __init__(self):
        self.ops = {e: [] for e in ENGINES}
        self.cnt = {}
        self.seen = {e: {} for e in ENGINES}
        self.dma_i = {"sp": 0, "pool": 0}
        self.pending = {e: False for e in ENGINES}
        for e in ENGINES:
            self.cnt["c_" + e] = 0
        for q in ("sp", "pool"):
            for i in range(NDSEM):
                self.cnt[f"d_{q}{i}"] = 0

    def _need(self, eng, ev, waits):
        if ev is None:
            return
        k, v = ev
        if self.seen[eng].get(k, 0) >= v:
            return
        self.seen[eng][k] = v
        waits[k] = max(waits.get(k, 0), v)

    def op(self, eng, fn, rd=(), wr=(), wr_more=(), dma=False, sig=True):
        waits = {}
        for b in rd:
            self._need(eng, b.w, waits)
        for b in wr:
            self._need(eng, b.w, waits)
            for ev in b.r:
                self._need(eng, ev, waits)
        if dma:
            i = self.dma_i[eng]
            self.dma_i[eng] = i + 1
            k = f"d_{eng}{i % NDSEM}"
            if self.cnt[k] > 0:
                self._need(eng, (k, self.cnt[k]), waits)
            self.cnt[k] += 16
            inc = (k, 16)
            ev = (k, self.cnt[k])
        else:
            k = "c_" + eng
            if sig:
                self.cnt[k] += 1
                inc = (k, 1)
                ev = (k, self.cnt[k])
                self.pending[eng] = False
            else:
                inc = None
                ev = (k, self.cnt[k] + 1)
                self.pending[eng] = True
        for b in rd:
            b.r.append(ev)
        for b in wr:
            b.w = ev
            b.r = []
        for b in wr_more:
            b.w = ev
        self.ops[eng].append((sorted(waits.items()), fn, inc))
        return ev

    def barrier(self):
        for e in ENGINES:
            assert not self.pending[e], e
        for e in ENGINES:
            waits = {}
            for k, v in self.cnt.items():
                if v > 0:
                    self._need(e, (k, v), waits)
            if waits:
                self.ops[e].append((sorted(waits.items()), None, None))

    def emit(self, nc):
        with contextlib.ExitStack() as st:
            sems = {k: st.enter_context(nc.semaphore(k)) for k in self.cnt}
            block = st.enter_context(nc.Block())

            def run(name):
                def body(eng):
                    for waits, fn, inc in self.ops[name]:
                        for k, v in waits:
                            eng.wait_ge(sems[k], v)
                        if fn is not None:
                            ins = fn(eng)
                            if inc is not None:
                                ins.then_inc(sems[inc[0]], inc[1])
                return body

            block.tensor(run("pe"))
            block.scalar(run("act"))
            block.vector(run("dve"))
            block.gpsimd(run("pool"))
            block.sync(run("sp"))


def t5_bucket(rel):
    nb = N_BUCKETS // 2
    exact = nb // 2
    n = np.abs(rel)
    sign = np.where(rel > 0, nb, 0)
    large = exact + (np.log(np.maximum(n, 1) / exact) / math.log(MAX_DISTANCE / exact) * (nb - exact)).astype(np.int64)
    large = np.minimum(large, nb - 1)
    return (sign + np.where(n < exact, n, large)).astype(np.int32)


def static_tables():
    u = np.arange(GA_W)
    o = 1151 - u
    ok = u <= 2302
    b = t5_bucket(o)
    ohA = np.zeros((32, GA_W), np.float32)
    ohA[b[ok], u[ok]] = 1.0
    ao = np.abs(o)
    mult = ((ao <= 64).astype(np.float32) + ((o % 4 == 0) & (ao <= 256)) + ((o % 16 == 0) & (ao <= 1024))).astype(np.float32)
    mult[~ok] = 0.0
    multA = np.tile(mult[None], (128, 1)).astype(np.float32)
    ub = np.arange(128)
    ohB = np.zeros((31, 128), np.float32)
    cidx = np.clip(63 - ub, -15, 15) + 15
    ohB[cidx[:127], ub[:127]] = 1.0
    qc = np.arange(64)
    cstart = np.clip(qc - 8, 0, 48)
    kcol = np.arange(64)
    cm = (kcol[:, None] >= cstart[None, :]) & (kcol[:, None] < cstart[None, :] + 16)
    cmaskT = np.concatenate([cm, cm], axis=0).astype(np.float32)
    ident = np.eye(128, dtype=np.float32)
    return dict(ohA=ohA, multA=multA, ohB=ohB, cmaskT=cmaskT, ident=ident)


def build(L, CH, debug=False):
    W0 = CH + 2 * HT * L
    WT = W0 * 128
    nc = bass.Bass("TRN2", target_bir_lowering=False)

    def din(name, shape, dtype=F32):
        return nc.dram_tensor(name, list(shape), dtype, kind="ExternalInput")

    def dscr(name, shape, dtype):
        return nc.dram_tensor(name, list(shape), dtype, kind="ExternalOutput" if debug else "Internal")

    xin = din("xin", [D, WT]).ap()
    valid_d = din("valid", [128, W0]).ap()
    flags_d = din("flags", [128, 2]).ap()
    cvec_d = din("cvec", [128, KC]).ap()
    w_in_d = din("w_in_r", [L, 12, 128, KC, 512]).ap()
    w_out_d = din("w_out_r", [L, 4, 128, KC, 512]).ap()
    w_gu_d = din("w_gu_r", [L, FC, 128, 2, KC, 128]).ap()
    w_dn_d = din("w_dn_r", [L, 2, KC, 128, FH, 128]).ap()
    w_ada_d = din("w_ada_r", [L, 12, 128, 8, KC, 128]).ap()
    b_ada_d = din("b_ada_r", [L, 128, 96]).ap()
    n1g_d = din("n1g_r", [L, 128, KC]).ap()
    n2g_d = din("n2g_r", [L, 128, KC]).ap()
    ong_d = din("ong_r", [L, 128, KC]).ap()
    fg_d = din("fg_r", [128, KC]).ap()
    t5_d = din("t5", [32, 8]).ap()
    rpbT_d = din("rpbT", [L, 31, 120]).ap()
    ohA_d = din("ohA", [32, GA_W]).ap()
    multA_d = din("multA", [128, GA_W]).ap()
    ohB_d = din("ohB", [31, 128]).ap()
    cmask_d = din("cmaskT", [128, 64]).ap()
    ident_d = din("ident", [128, 128]).ap()
    yout = nc.dram_tensor("yout", [D, CH * 128], F32, kind="ExternalOutput").ap()

    xs = dscr("xs", [D, WT], F32).ap()
    qTd = dscr("qTd", [NH, 128, WT], BF16).ap()
    kTd = dscr("kTd", [NH, 128, WT], BF16).ap()
    Vd = dscr("Vd", [NH, 128, W0, 130], BF16).ap()
    GdA_t = dscr("GdA", [8, 128, GA_W], BF16)
    GdA = GdA_t.ap()
    GdB_t = dscr("GdB", [8, 128, GB_W], BF16)
    GdB = GdB_t.ap()
    CCd = dscr("CCd", [L, 8, 128, 2 * 896], BF16).ap()

    p = Prog()
    es = contextlib.ExitStack()

    uid = [0]

    def sb(stack, name, shape, dtype):
        uid[0] += 1
        return stack.enter_context(nc.sbuf_tensor(f"s{uid[0]}_{name}", list(shape), dtype))

    def ps(stack, name, shape, dtype):
        uid[0] += 1
        return stack.enter_context(nc.psum_tensor(f"p{uid[0]}_{name}", list(shape), dtype))

    def dma(q, out, in_, rd=(), wr=()):
        p.op(q, lambda e: e.dma_start(out=out, in_=in_), rd=rd, wr=wr, dma=True)

    def mm(out, lhsT, rhs, start, stop, rd, wb, first, sig):
        p.op("pe", lambda e: e.matmul(out, lhsT=lhsT, rhs=rhs, start=start, stop=stop),
             rd=rd, wr=[wb] if first else (), wr_more=() if first else [wb], sig=sig)

    def act(out, in_, func, rd, wr, scale=None, bias=None, wr_more=()):
        kw = {}
        if scale is not None:
            kw["scale"] = scale
        if bias is not None:
            kw["bias"] = bias
        p.op("act", lambda e: e.activation(out=out, in_=in_, func=func, **kw), rd=rd, wr=wr, wr_more=wr_more)

    def actmul(out, in_, m, rd, wr, wr_more=()):
        p.op("act", lambda e: e.mul(out=out, in_=in_, mul=m), rd=rd, wr=wr, wr_more=wr_more)

    def actcopy(out, in_, rd, wr, wr_more=()):
        p.op("act", lambda e: e.copy(out=out, in_=in_), rd=rd, wr=wr, wr_more=wr_more)

    def tt(eng, out, in0, in1, op, rd, wr, wr_more=()):
        p.op(eng, lambda e: e.tensor_tensor(out=out, in0=in0, in1=in1, op=op), rd=rd, wr=wr, wr_more=wr_more)

    def ts(eng, out, in0, s1, op0, rd, wr, s2=None, op1=None, wr_more=()):
        if op1 is None:
            p.op(eng, lambda e: e.tensor_scalar(out=out, in0=in0, scalar1=s1, scalar2=None, op0=op0),
                 rd=rd, wr=wr, wr_more=wr_more)
        else:
            p.op(eng, lambda e: e.tensor_scalar(out=out, in0=in0, scalar1=s1, scalar2=s2, op0=op0, op1=op1),
                 rd=rd, wr=wr, wr_more=wr_more)

    def stt(eng, out, in0, scalar, in1, op0, op1, rd, wr, wr_more=()):
        p.op(eng, lambda e: e.scalar_tensor_tensor(out=out, in0=in0, scalar=scalar, in1=in1, op0=op0, op1=op1),
             rd=rd, wr=wr, wr_more=wr_more)

    def recip(out, in_, rd, wr):
        p.op("dve", lambda e: e.reciprocal(out=out, in_=in_), rd=rd, wr=wr)

    def copy(eng, out, in_, rd, wr, wr_more=()):
        p.op(eng, lambda e: e.tensor_copy(out=out, in_=in_), rd=rd, wr=wr, wr_more=wr_more)

    def memset(eng, ap, val, wr):
        p.op(eng, lambda e: e.memset(ap, val), wr=wr)

    def transpose(out, in_, ident, rd, wb):
        p.op("pe", lambda e: e.transpose(out, in_, ident), rd=rd, wr=[wb])

    ones_bf = sb(es, "ones_bf", [128, 128], BF16)
    ident_bf = sb(es, "ident_bf", [128, 128], BF16)
    eps_t = sb(es, "eps_t", [128, 1], F32)
    tiny_t = sb(es, "tiny_t", [128, 1], F32)
    valid_s = sb(es, "valid_s", [128, W0], F32)
    flags_s = sb(es, "flags_s", [128, 2], F32)
    modc = sb(es, "modc", [128, L, 96], F32)
    gm = sb(es, "gm", [128, L, 32], F32)
    ong_s = sb(es, "ong_s", [128, L, KC], F32)
    fg_s = sb(es, "fg_s", [128, KC], F32)
    onescol = sb(es, "onescol", [128, 4, 1], F32)
    B_const = Buf("const")

    mm_ps = [ps(es, f"mm{i}", [128, 512], F32) for i in range(2)]
    mu_ps = [ps(es, f"mu{i}", [128, 512], F32) for i in range(2)]
    st_ps = [ps(es, f"st{i}", [128, 512], F32) for i in range(2)]
    pv_ps = ps(es, "pv", [128, 2, 256], F32)
    tp_ps = ps(es, "tp", [128, 2, 512], BF16)
    B_mm = [Buf(f"mm{i}") for i in range(2)]
    B_mu = [Buf(f"mu{i}") for i in range(2)]
    B_st = [Buf(f"st{i}") for i in range(2)]
    B_pv = [Buf(f"pv{i}") for i in range(2)]
    B_tp = [Buf(f"tp{i}") for i in range(2)]

    actT = sb(es, "actT", [128, KC, BT * 128], BF16)
    B_act = Buf("actT")
    slab = [sb(es, f"slab{i}", [128, KC, 512], BF16) for i in range(2)]
    B_slab = [Buf(f"slab{i}") for i in range(2)]
    slab_i = [0]

    def next_slab():
        i = slab_i[0] % 2
        slab_i[0] += 1
        return slab[i], B_slab[i]

    rot = {}

    def rotate(key, n):
        i = rot.get(key, 0)
        rot[key] = i + 1
        return i % n

    memset("pool", ones_bf[:], 1.0, [B_const])
    memset("pool", eps_t[:], EPS, [B_const])
    memset("pool", tiny_t[:], 1e-30, [B_const])
    memset("pool", onescol[:], 1.0, [B_const])
    dma("pool", ident_bf[:], ident_d[:, :], wr=[B_const])
    dma("sp", valid_s[:], valid_d[:, :], wr=[B_const])
    dma("sp", flags_s[:], flags_d[:, :], wr=[B_const])
    dma("sp", fg_s[:], fg_d[:, :], wr=[B_const])
    for l in range(L):
        dma("sp", ong_s[:, l, :], ong_d[l], wr=[B_const])
    p.barrier()

    with contextlib.ExitStack() as s0:
        cv = sb(s0, "cv", [128, KC], F32)
        csil = sb(s0, "csil", [128, KC], BF16)
        bada = sb(s0, "bada", [128, L, 96], F32)
        n1g = sb(s0, "n1g", [128, L, KC], F32)
        n2g = sb(s0, "n2g", [128, L, KC], F32)
        wa = [sb(s0, f"wa{i}", [128, 8, KC, 128], BF16) for i in range(2)]
        B_wa = [Buf("wa0"), Buf("wa1")]
        B_c0 = Buf("c0")
        dma("sp", cv[:], cvec_d[:, :], wr=[B_c0])
        for l in range(L):
            dma("sp", bada[:, l, :], b_ada_d[l], wr=[B_c0])
            dma("sp", n1g[:, l, :], n1g_d[l], wr=[B_c0])
            dma("sp", n2g[:, l, :], n2g_d[l], wr=[B_c0])
        p.barrier()
        act(csil[:], cv[:], AF.Silu, rd=[B_c0], wr=[B_c0])
        for l in range(L):
            for jg in range(12):
                i = (l * 12 + jg) % 2
                dma("pool", wa[i][:], w_ada_d[l, jg], wr=[B_wa[i]])
                for j in range(8):
                    col = jg * 8 + j
                    for kc in range(KC):
                        mm(st_ps[0][:, col:col + 1], wa[i][:, j, kc, :], csil[:, kc:kc + 1],
                           kc == 0, kc == KC - 1, rd=[B_wa[i], B_c0], wb=B_st[0],
                           first=(jg == 0 and j == 0 and kc == 0), sig=(kc == KC - 1))
            tt("dve", modc[:, l, :], st_ps[0][:, 0:96], bada[:, l, :], ALU.add, rd=[B_st[0], B_c0], wr=[B_const])
            stt("dve", gm[:, l, 0:16], modc[:, l, 16:32], 1.0, n1g[:, l, :], ALU.add, ALU.mult, rd=[B_const, B_c0], wr=[B_const])
            stt("dve", gm[:, l, 16:32], modc[:, l, 64:80], 1.0, n2g[:, l, :], ALU.add, ALU.mult, rd=[B_const, B_c0], wr=[B_const])
        p.barrier()

    with contextlib.ExitStack() as s0:
        t5s = sb(s0, "t5s", [32, 8], F32)
        ohA = sb(s0, "ohA", [32, GA_W], BF16)
        multA = sb(s0, "multA", [128, GA_W], F32)
        ones32 = sb(s0, "ones32", [32, 128], F32)
        lh = [sb(s0, f"lh{i}", [32, 128], BF16) for i in range(2)]
        gA = [sb(s0, f"gA{i}", [128, GA_W], BF16) for i in range(2)]
        etmp = [sb(s0, f"etmp{i}", [128, 512], F32) for i in range(2)]
        B_m = Buf("m")
        B_lh = [Buf("lh0"), Buf("lh1")]
        B_gA = [Buf("gA0"), Buf("gA1")]
        B_et = [Buf("et0"), Buf("et1")]
        dma("sp", t5s[:], t5_d[:, :], wr=[B_m])
        dma("pool", ohA[:], ohA_d[:, :], wr=[B_m])
        dma("sp", multA[:], multA_d[:, :], wr=[B_m])
        memset("pool", ones32[:], 1.0, [B_m])
        p.barrier()
        for h in range(8):
            i = h % 2
            ts("dve", lh[i][:], ones32[:], t5s[:, h:h + 1], ALU.mult, rd=[B_m], wr=[B_lh[i]])
            for c in range(5):
                n = min(512, GA_W - c * 512)
                j = rotate("pA", 2)
                mm(mm_ps[j][:, 0:n], lh[i][:], ohA[:, c * 512:c * 512 + n], True, True, rd=[B_lh[i], B_m], wb=B_mm[j], first=True, sig=True)
                act(etmp[j][:, 0:n], mm_ps[j][:, 0:n], AF.Exp, rd=[B_mm[j]], wr=[B_et[j]])
                tt("dve", gA[i][:, c * 512:c * 512 + n], etmp[j][:, 0:n], multA[:, c * 512:c * 512 + n], ALU.mult,
                   rd=[B_et[j], B_m], wr=[B_gA[i]] if c == 0 else (), wr_more=() if c == 0 else [B_gA[i]])
            dma("sp", GdA[h], gA[i][:], rd=[B_gA[i]])
        p.barrier()

    with contextlib.ExitStack() as s0:
        rpbs = sb(s0, "rpbs", [31, L, 120], F32)
        ohB = sb(s0, "ohB", [31, 128], BF16)
        ones31 = sb(s0, "ones31", [31, 128], F32)
        cmask = sb(s0, "cmask", [128, 64], F32)
        lhb = [sb(s0, f"lhb{i}", [31, 128], BF16) for i in range(2)]
        gB = [sb(s0, f"gB{i}", [128, GB_W], BF16) for i in range(2)]
        ccf = [sb(s0, f"ccf{i}", [128, 14, 64], BF16) for i in range(2)]
        ccm = [sb(s0, f"ccm{i}", [128, 14, 64], BF16) for i in range(2)]
        cc2 = [sb(s0, f"cc2{i}", [128, 2, 14, 64], BF16) for i in range(2)]
        B_m = Buf("mB")
        B_lhb = [Buf("lhb0"), Buf("lhb1")]
        B_gB = [Buf("gB0"), Buf("gB1")]
        B_ccf = [[Buf("ccf00"), Buf("ccf01")], [Buf("ccf10"), Buf("ccf11")]]
        B_ccm = [Buf("ccm0"), Buf("ccm1")]
        B_cc2 = [Buf("cc20"), Buf("cc21")]
        B_GdB = [Buf(f"GdB{h}") for h in range(8)]
        for l in range(L):
            dma("sp", rpbs[:, l, :], rpbT_d[l], wr=[B_m])
        dma("pool", ohB[:], ohB_d[:, :], wr=[B_m])
        dma("sp", cmask[:], cmask_d[:, :], wr=[B_m])
        memset("pool", ones31[:], 1.0, [B_m])
        p.barrier()
        for l in range(L):
            for h in range(8):
                i = (l * 8 + h) % 2
                for sp_ in range(15):
                    jb = rotate("lhb", 2)
                    ts("dve", lhb[jb][:], ones31[:], rpbs[:, l, h * 15 + sp_:h * 15 + sp_ + 1], ALU.mult, rd=[B_m], wr=[B_lhb[jb]])
                    if sp_ % 4 == 0:
                        j = rotate("pA", 2)
                    mm(mm_ps[j][:, (sp_ % 4) * 128:(sp_ % 4 + 1) * 128], lhb[jb][:], ohB[:], True, True,
                       rd=[B_lhb[jb], B_m], wb=B_mm[j], first=(sp_ % 4 == 0), sig=True)
                    if sp_ % 4 == 3 or sp_ == 14:
                        s0_ = (sp_ // 4) * 4
                        n = (sp_ - s0_ + 1) * 128
                        act(gB[i][:, s0_ * 128:s0_ * 128 + n], mm_ps[j][:, 0:n], AF.Exp, rd=[B_mm[j]],
                            wr=[B_gB[i]] if s0_ == 0 else (), wr_more=() if s0_ == 0 else [B_gB[i]])
                dma("sp", GdB[h], gB[i][:], rd=[B_gB[i]], wr=[B_GdB[h]])
                for kr in range(2):
                    src = bass.AP(GdB_t, h * 128 * GB_W + (1 - kr) * 128 + 63, [[GB_W - 1, 64], [128, 14], [1, 64]])
                    dma("sp", ccf[i][kr * 64:(kr + 1) * 64, :, :], src, rd=[B_GdB[h]], wr=[B_ccf[i][kr]])
                for s_ in range(14):
                    tt("dve", ccm[i][:, s_, :], ccf[i][:, s_, :], cmask[:], ALU.mult, rd=[B_ccf[i][0], B_ccf[i][1], B_m],
                       wr=[B_ccm[i]] if s_ == 0 else (), wr_more=() if s_ == 0 else [B_ccm[i]])
                memset("pool", cc2[i][:], 0.0, [B_cc2[i]])
                cps = [(0, 0, 3, 11), (64, 0, 4, 12), (0, 1, 0, 3), (0, 1, 11, 14), (64, 1, 0, 4), (64, 1, 12, 14)]
                for (p0, w, a, b_) in cps:
                    copy("dve", cc2[i][p0:p0 + 64, w, a:b_, :], ccm[i][p0:p0 + 64, a:b_, :], rd=[B_ccm[i], B_cc2[i]], wr=(), wr_more=[B_cc2[i]])
                dma("sp", CCd[l, h], cc2[i][:], rd=[B_cc2[i]])
        p.barrier()

    def norm_block(l, which, src_ap, tok0, xg, B_xg, sq, B_sq, tmpn, B_tmpn, rstd, B_rstd, g, load=True):
        if load:
            dma("sp", xg[:], src_ap[:, tok0:tok0 + 512].rearrange("(c p) t -> p c t", p=128), wr=[B_xg])
        tt("pool", sq[:], xg[:], xg[:], ALU.mult, rd=[B_xg], wr=[B_sq])
        for kc in range(KC):
            mm(st_ps[0][:, :], ones_bf[:], sq[:, kc, :], kc == 0, kc == KC - 1, rd=[B_sq, B_const], wb=B_st[0], first=(kc == 0), sig=(kc == KC - 1))
        act(rstd[:], st_ps[0][:, :], AF.Sqrt, rd=[B_st[0], B_const], wr=[B_rstd], scale=1.0 / D, bias=eps_t[:])
        recip(rstd[:], rstd[:], rd=[B_rstd], wr=[B_rstd])
        goff = 0 if which == 1 else 16
        soff = 0 if which == 1 else 48
        for kc in range(KC):
            j = rotate("tmpn", 2)
            tt("dve", tmpn[j][:], xg[:, kc, :], rstd[:], ALU.mult, rd=[B_xg, B_rstd], wr=[B_tmpn[j]])
            act(actT[:, kc, g * 512:(g + 1) * 512], tmpn[j][:], AF.Identity, rd=[B_tmpn[j], B_const],
                wr=[B_act] if (kc == 0 and g == 0) else (), wr_more=() if (kc == 0 and g == 0) else [B_act],
                scale=gm[:, l, goff + kc:goff + kc + 1], bias=modc[:, l, soff + kc:soff + kc + 1])

    for l in range(L):
        kv0, kv1 = HT * l, W0 - HT * l
        o0, o1 = HT * (l + 1), W0 - HT * (l + 1)
        xsrc = xin if l == 0 else xs

        with contextlib.ExitStack() as s1:
            xg = [sb(s1, f"xg{i}", [128, KC, 512], F32) for i in range(2)]
            B_xg = [Buf("xg0"), Buf("xg1")]
            sq = sb(s1, "sq", [128, KC, 512], BF16)
            B_sq = Buf("sq")
            tmpn = [sb(s1, f"tmpn{i}", [128, 512], F32) for i in range(2)]
            B_tmpn = [Buf("tmpn0"), Buf("tmpn1")]
            rstd = sb(s1, "rstd", [128, 512], F32)
            B_rstd = Buf("rstd")
            kst = [sb(s1, f"kst{i}", [128, 512], BF16) for i in range(3)]
            B_kst = [Buf(f"kst{i}") for i in range(3)]
            vst = [sb(s1, f"vst{i}", [128, 4, BT, 130], BF16) for i in range(2)]
            B_vst = [Buf("vst0"), Buf("vst1")]
            B_vone = [Buf("vone0"), Buf("vone1")]
            for b0 in range(kv0, kv1, BT):
                tok0 = b0 * 128
                need_q = (o0 <= b0 < o1)
                for g in range(2):
                    norm_block(l, 1, xsrc, tok0 + g * 512, xg[g], B_xg[g], sq, B_sq, tmpn, B_tmpn, rstd, B_rstd, g)
                for s in range(12):
                    kind = ("q", "k", "v")[(s % 6) // 2]
                    if kind == "q" and not need_q:
                        continue
                    sl, B_sl = next_slab()
                    dma("pool", sl[:], w_in_d[l, s], wr=[B_sl])
                    hbase = (s // 6) * 8 + (s % 2) * 4
                    if kind in ("q", "k"):
                        dst = qTd if kind == "q" else kTd
                        for cc in range(4):
                            for g in range(2):
                                j = rotate("mm", 2)
                                for kc in range(KC):
                                    mm(mm_ps[j][:, :], sl[:, kc, cc * 128:(cc + 1) * 128], actT[:, kc, g * 512:(g + 1) * 512],
                                       kc == 0, kc == KC - 1, rd=[B_sl, B_act], wb=B_mm[j], first=(kc == 0), sig=(kc == KC - 1))
                                ks = rotate("kst", 3)
                                if kind == "q":
                                    actmul(kst[ks][:], mm_ps[j][:, :], QK_SCALE, rd=[B_mm[j]], wr=[B_kst[ks]])
                                else:
                                    copy("dve", kst[ks][:], mm_ps[j][:, :], rd=[B_mm[j]], wr=[B_kst[ks]])
                                dma("sp", dst[hbase + cc, :, tok0 + g * 512:tok0 + (g + 1) * 512], kst[ks][:], rd=[B_kst[ks]])
                    else:
                        vi = rotate("vst", 2)
                        for t in range(BT):
                            j = rotate("mm", 2)
                            for kc in range(KC):
                                mm(mm_ps[j][:, :], actT[:, kc, t * 128:(t + 1) * 128], sl[:, kc, :],
                                   kc == 0, kc == KC - 1, rd=[B_sl, B_act], wb=B_mm[j], first=(kc == 0), sig=(kc == KC - 1))
                            vcol = valid_s[:, b0 + t:b0 + t + 1]
                            ts("dve", vst[vi][:, :, t, 0:128], mm_ps[j][:, :].rearrange("p (h d) -> p h d", h=4), vcol, ALU.mult,
                               rd=[B_mm[j], B_const], wr=[B_vst[vi]] if t == 0 else (), wr_more=() if t == 0 else [B_vst[vi]])
                            ts("pool", vst[vi][:, :, t, 128:129], onescol[:], vcol, ALU.mult, rd=[B_const],
                               wr=[B_vone[vi]] if t == 0 else (), wr_more=() if t == 0 else [B_vone[vi]])
                        for hh in range(4):
                            dma("sp", Vd[hbase + hh, :, b0:b0 + BT, :], vst[vi][:, hh, :, :], rd=[B_vst[vi], B_vone[vi]])
            p.barrier()

        for q0 in range(o0, o1, BT):
            tok0 = q0 * 128
            spec = {}
            c0 = HT * L
            spec[c0] = ("s0", 0)
            spec[c0 + 1] = ("s1", 0)
            spec[c0 + CH - 2] = ("e1", 1)
            spec[c0 + CH - 1] = ("e0", 1)
            with contextlib.ExitStack() as s3:
                kwin = [sb(s3, f"kwin{i}", [128, 24 * 128], BF16) for i in range(2)]
                vwin = [sb(s3, f"vwin{i}", [128, 24, 130], BF16) for i in range(2)]
                qwin = [sb(s3, f"qwin{i}", [128, BT * 128], BF16) for i in range(2)]
                em = [sb(s3, f"em{i}", [128, 2176], BF16) for i in range(2)]
                mb = [sb(s3, f"mb{i}", [128, 896], BF16) for i in range(2)]
                Pb = [sb(s3, f"Pb{i}", [128, 512], BF16) for i in range(2)]
                Pm = [sb(s3, f"Pm{i}", [128, 512], BF16) for i in range(3)]
                yst = [sb(s3, f"yst{i}", [128, 128], BF16) for i in range(2)]
                rden = [sb(s3, f"rden{i}", [128, 1], F32) for i in range(2)]
                B_kw = [Buf("kw0"), Buf("kw1")]
                B_vw = [Buf("vw0"), Buf("vw1")]
                B_qw = [Buf("qw0"), Buf("qw1")]
                B_em = [Buf("em0"), Buf("em1")]
                B_mb = [Buf("mb0"), Buf("mb1")]
                B_Pb = [Buf("Pb0"), Buf("Pb1")]
                B_Pm = [Buf("Pm0"), Buf("Pm1"), Buf("Pm2")]
                B_yst = [Buf("yst0"), Buf("yst1")]
                B_rd = [Buf("rd0"), Buf("rd1")]
                first_act = [True]
                for hd in range(NH):
                    isA = hd < 8
                    hk = 8 if isA else 3
                    nw = BT + 2 * hk
                    bi = hd % 2
                    dma("sp", kwin[bi][:, 0:nw * 128], kTd[hd, :, (q0 - hk) * 128:(q0 + BT + hk) * 128], wr=[B_kw[bi]])
                    dma("sp", vwin[bi][:, 0:nw, :], Vd[hd, :, q0 - hk:q0 + BT + hk, :], wr=[B_vw[bi]])
                    dma("sp", qwin[bi][:], qTd[hd, :, tok0:tok0 + BT * 128], wr=[B_qw[bi]])
                    if isA:
                        src = bass.AP(GdA_t, hd * 128 * GA_W + 127, [[GA_W - 1, 128], [1, 2176]])
                        dma("sp", em[bi][:], src, wr=[B_em[bi]])
                    else:
                        dma("sp", em[bi][:, 0:1792], CCd[l, hd - 8], wr=[B_em[bi]])
                    mb_ready = {}
                    for t in range(BT):
                        qt = q0 + t
                        if isA:
                            groups = [[8, 7, 6, 5], [4, 3, 2, 1], [0, -1, -2, -3], [-4, -5, -6, -7], [-8]]
                            msrc = lambda dmax, n: (em[bi][:, 128 * (8 - dmax):128 * (8 - dmax) + 128 * n], B_em[bi])
                        else:
                            kind = spec.get(qt)
                            if kind is None:
                                groups = [[2, 1, 0, -1], [-2]]
                                msrc = lambda dmax, n: (em[bi][:, (6 - 2 * dmax) * 64:(6 - 2 * dmax) * 64 + 128 * n], B_em[bi])
                            else:
                                nm, fl = kind
                                if fl not in mb_ready:
                                    mi = rotate("mb", 2)
                                    stt("dve", mb[mi][:], em[bi][:, 896:1792], flags_s[:, fl:fl + 1], em[bi][:, 0:896],
                                        ALU.mult, ALU.add, rd=[B_em[bi], B_const], wr=[B_mb[mi]])
                                    mb_ready[fl] = mi
                                mi = mb_ready[fl]
                                groups = {"s0": [[3, 2, 1, 0], [-1, -2]], "s1": [[2, 1, 0, -1], [-2]],
                                          "e1": [[2, 1, 0, -1], [-2]], "e0": [[2, 1, 0, -1], [-2, -3]]}[nm]
                                msrc = (lambda mi_: (lambda dmax, n: (mb[mi_][:, (6 - 2 * dmax) * 64:(6 - 2 * dmax) * 64 + 128 * n], B_mb[mi_])))(mi)
                        vb = rotate("pv", 2)
                        ntot = sum(len(gp) for gp in groups)
                        done = 0
                        for gp in groups:
                            n = len(gp)
                            sbk = rotate("mm", 2)
                            for i_, dl in enumerate(gp):
                                kw_ = t + hk + dl
                                mm(mm_ps[sbk][:, i_ * 128:(i_ + 1) * 128], kwin[bi][:, kw_ * 128:(kw_ + 1) * 128],
                                   qwin[bi][:, t * 128:(t + 1) * 128], True, True, rd=[B_kw[bi], B_qw[bi]], wb=B_mm[sbk],
                                   first=(i_ == 0), sig=(i_ == n - 1))
                            pb = rotate("Pb", 2)
                            act(Pb[pb][:, 0:n * 128], mm_ps[sbk][:, 0:n * 128], AF.Exp, rd=[B_mm[sbk]], wr=[B_Pb[pb]])
                            pm = rotate("Pm", 3)
                            map_, B_map = msrc(gp[0], n)
                            tt("dve", Pm[pm][:, 0:n * 128], Pb[pb][:, 0:n * 128], map_, ALU.mult, rd=[B_Pb[pb], B_map], wr=[B_Pm[pm]])
                            for i_, dl in enumerate(gp):
                                kw_ = t + hk + dl
                                mm(pv_ps[:, vb, 0:129], Pm[pm][:, i_ * 128:(i_ + 1) * 128], vwin[bi][:, kw_, 0:129],
                                   done == 0, done == ntot - 1, rd=[B_Pm[pm], B_vw[bi]], wb=B_pv[vb],
                                   first=(done == 0), sig=(done == ntot - 1))
                                done += 1
                        ri = rotate("rden", 2)
                        ts("dve", rden[ri][:], pv_ps[:, vb, 128:129], tiny_t[:, 0:1], ALU.max, rd=[B_pv[vb], B_const], wr=[B_rd[ri]])
                        recip(rden[ri][:], rden[ri][:], rd=[B_rd[ri]], wr=[B_rd[ri]])
                        yi = rotate("yst", 2)
                        ts("dve", yst[yi][:], pv_ps[:, vb, 0:128], rden[ri][:, 0:1], ALU.mult, rd=[B_pv[vb], B_rd[ri]], wr=[B_yst[yi]])
                        ti = rotate("tp", 2)
                        transpose(tp_ps[:, ti, 0:128], yst[yi][:], ident_bf[:], rd=[B_yst[yi], B_const], wb=B_tp[ti])
                        actcopy(actT[:, hd, t * 128:(t + 1) * 128], tp_ps[:, ti, 0:128], rd=[B_tp[ti]],
                                wr=[B_act] if first_act[0] else (), wr_more=() if first_act[0] else [B_act])
                        first_act[0] = False
                p.barrier()

            with contextlib.ExitStack() as s4:
                xnew = sb(s4, "xnew", [128, 2, KC, 512], F32)
                B_xn = [Buf("xn0"), Buf("xn1")]
                sqy = sb(s4, "sqy", [128, KC, 512], BF16)
                B_sqy = Buf("sqy")
                rs = [sb(s4, f"rs{i}", [128, 512], F32) for i in range(2)]
                B_rs = [Buf("rs0"), Buf("rs1")]
                xp = [sb(s4, f"xp{i}", [128, 512], F32) for i in range(3)]
                B_xp = [Buf(f"xp{i}") for i in range(3)]
                sqp = [sb(s4, f"sqp{i}", [128, 512], BF16) for i in range(2)]
                B_sqp = [Buf("sqp0"), Buf("sqp1")]
                tmpn = [sb(s4, f"tmq{i}", [128, 512], F32) for i in range(2)]
                B_tmpn = [Buf("tmq0"), Buf("tmq1")]
                rstd2 = [sb(s4, f"rstd2{i}", [128, 512], F32) for i in range(2)]
                B_rstd2 = [Buf("r20"), Buf("r21")]
                for g in range(2):
                    tt("pool", sqy[:], actT[:, :, g * 512:(g + 1) * 512], actT[:, :, g * 512:(g + 1) * 512], ALU.mult, rd=[B_act], wr=[B_sqy])
                    for half in range(2):
                        for kk in range(8):
                            kc = half * 8 + kk
                            mm(st_ps[half][:, :], ones_bf[:], sqy[:, kc, :], kk == 0, kk == 7, rd=[B_sqy, B_const], wb=B_st[half], first=(kk == 0), sig=(kk == 7))
                        act(rs[half][:], st_ps[half][:, :], AF.Sqrt, rd=[B_st[half], B_const], wr=[B_rs[half]], scale=1.0 / 1024, bias=eps_t[:])
                        recip(rs[half][:], rs[half][:], rd=[B_rs[half]], wr=[B_rs[half]])
                    for kc in range(KC):
                        half = kc // 8
                        stt("dve", actT[:, kc, g * 512:(g + 1) * 512], actT[:, kc, g * 512:(g + 1) * 512], ong_s[:, l, kc:kc + 1], rs[half][:],
                            ALU.mult, ALU.mult, rd=[B_rs[half], B_const, B_act], wr=[B_act] if kc == 0 else (), wr_more=() if kc == 0 else [B_act])
                for s in range(4):
                    sl, B_sl = next_slab()
                    dma("pool", sl[:], w_out_d[l, s], wr=[B_sl])
                    for cc in range(4):
                        ch = s * 4 + cc
                        for g in range(2):
                            j = rotate("mm", 2)
                            for kc in range(KC):
                                mm(mm_ps[j][:, :], sl[:, kc, cc * 128:(cc + 1) * 128], actT[:, kc, g * 512:(g + 1) * 512],
                                   kc == 0, kc == KC - 1, rd=[B_sl, B_act], wb=B_mm[j], first=(kc == 0), sig=(kc == KC - 1))
                            xi = rotate("xp", 3)
                            dma("sp", xp[xi][:], xsrc[ch * 128:(ch + 1) * 128, tok0 + g * 512:tok0 + (g + 1) * 512], wr=[B_xp[xi]])
                            stt("dve", xnew[:, g, ch, :], mm_ps[j][:, :], modc[:, l, 32 + ch:33 + ch], xp[xi][:], ALU.mult, ALU.add,
                                rd=[B_mm[j], B_xp[xi], B_const], wr=[B_xn[g]] if ch == 0 else (), wr_more=() if ch == 0 else [B_xn[g]])
                            dma("sp", xs[ch * 128:(ch + 1) * 128, tok0 + g * 512:tok0 + (g + 1) * 512], xnew[:, g, ch, :], rd=[B_xn[g]])
                            si = rotate("sqp", 2)
                            tt("pool", sqp[si][:], xnew[:, g, ch, :], xnew[:, g, ch, :], ALU.mult, rd=[B_xn[g]], wr=[B_sqp[si]])
                            mm(st_ps[g][:, :], ones_bf[:], sqp[si][:], ch == 0, ch == KC - 1, rd=[B_sqp[si], B_const], wb=B_st[g], first=(ch == 0), sig=True)
                for g in range(2):
                    act(rstd2[g][:], st_ps[g][:, :], AF.Sqrt, rd=[B_st[g], B_const], wr=[B_rstd2[g]], scale=1.0 / D, bias=eps_t[:])
                    recip(rstd2[g][:], rstd2[g][:], rd=[B_rstd2[g]], wr=[B_rstd2[g]])
                    for kc in range(KC):
                        j = rotate("tmq", 2)
                        tt("dve", tmpn[j][:], xnew[:, g, kc, :], rstd2[g][:], ALU.mult, rd=[B_xn[g], B_rstd2[g]], wr=[B_tmpn[j]])
                        act(actT[:, kc, g * 512:(g + 1) * 512], tmpn[j][:], AF.Identity, rd=[B_tmpn[j], B_const],
                            wr=[B_act] if (kc == 0 and g == 0) else (), wr_more=() if (kc == 0 and g == 0) else [B_act],
                            scale=gm[:, l, 16 + kc:17 + kc], bias=modc[:, l, 48 + kc:49 + kc])
                p.barrier()

            with contextlib.ExitStack() as s6:
                aT = sb(s6, "aT", [128, FH, BT * 128], BF16)
                B_aT = Buf("aT")
                wgu = [sb(s6, f"wgu{i}", [128, 2, KC, 128], BF16) for i in range(2)]
                B_wgu = [Buf("wgu0"), Buf("wgu1")]
                wd = [sb(s6, f"wd{i}", [128, FH, 128], BF16) for i in range(2)]
                B_wd = [Buf("wd0"), Buf("wd1")]
                sg = [sb(s6, f"sg{i}", [128, 512], F32) for i in range(2)]
                B_sg = [Buf("sg0"), Buf("sg1")]
                xp = [sb(s6, f"xq{i}", [128, 512], F32) for i in range(3)]
                B_xp = [Buf(f"xq{i}") for i in range(3)]
                xo = [sb(s6, f"xo{i}", [128, 512], F32) for i in range(3)]
                B_xo = [Buf(f"xo{i}") for i in range(3)]
                B_piece = [[Buf(f"pc{c}_{g}") for g in range(2)] for c in range(KC)]
                for fh in range(2):
                    for fi in range(FH):
                        ffc = fh * FH + fi
                        wi = rotate("wgu", 2)
                        dma("pool", wgu[wi][:], w_gu_d[l, ffc], wr=[B_wgu[wi]])
                        for g in range(2):
                            j = rotate("mm", 2)
                            for kc in range(KC):
                                mm(mm_ps[j][:, :], wgu[wi][:, 0, kc, :], actT[:, kc, g * 512:(g + 1) * 512], kc == 0, kc == KC - 1,
                                   rd=[B_wgu[wi], B_act], wb=B_mm[j], first=(kc == 0), sig=(kc == KC - 1))
                            ju = rotate("mu", 2)
                            for kc in range(KC):
                                mm(mu_ps[ju][:, :], wgu[wi][:, 1, kc, :], actT[:, kc, g * 512:(g + 1) * 512], kc == 0, kc == KC - 1,
                                   rd=[B_wgu[wi], B_act], wb=B_mu[ju], first=(kc == 0), sig=(kc == KC - 1))
                            gi = rotate("sg", 2)
                            act(sg[gi][:], mm_ps[j][:, :], AF.Silu, rd=[B_mm[j]], wr=[B_sg[gi]])
                            tt("dve", aT[:, fi, g * 512:(g + 1) * 512], sg[gi][:], mu_ps[ju][:, :], ALU.mult, rd=[B_sg[gi], B_mu[ju]],
                               wr=[B_aT] if (fi == 0 and g == 0) else (), wr_more=() if (fi == 0 and g == 0) else [B_aT])
                    for ch in range(KC):
                        wi = rotate("wd", 2)
                        dma("pool", wd[wi][:], w_dn_d[l, fh, ch], wr=[B_wd[wi]])
                        for g in range(2):
                            j = rotate("mm", 2)
                            for fi in range(FH):
                                mm(mm_ps[j][:, :], wd[wi][:, fi, :], aT[:, fi, g * 512:(g + 1) * 512], fi == 0, fi == FH - 1,
                                   rd=[B_wd[wi], B_aT], wb=B_mm[j], first=(fi == 0), sig=(fi == FH - 1))
                            xi = rotate("xq", 3)
                            xsl = xs[ch * 128:(ch + 1) * 128, tok0 + g * 512:tok0 + (g + 1) * 512]
                            dma("sp", xp[xi][:], xsl, rd=[B_piece[ch][g]], wr=[B_xp[xi]])
                            oi = rotate("xo", 3)
                            stt("dve", xo[oi][:], mm_ps[j][:, :], modc[:, l, 80 + ch:81 + ch], xp[xi][:], ALU.mult, ALU.add,
                                rd=[B_mm[j], B_xp[xi], B_const], wr=[B_xo[oi]])
                            dma("sp", xsl, xo[oi][:], rd=[B_xo[oi]], wr=[B_piece[ch][g]])
                p.barrier()

    with contextlib.ExitStack() as s8:
        xg = [sb(s8, f"fxg{i}", [128, KC, 512], F32) for i in range(2)]
        B_xg = [Buf("fxg0"), Buf("fxg1")]
        sq = sb(s8, "fsq", [128, KC, 512], BF16)
        B_sq = Buf("fsq")
        rstd = [sb(s8, f"frstd{i}", [128, 512], F32) for i in range(2)]
        B_rstd = [Buf("fr0"), Buf("fr1")]
        c0 = HT * L
        for gi in range(CH // 4):
            tok0 = (c0 + gi * 4) * 128
            i = gi % 2
            dma("sp", xg[i][:], xs[:, tok0:tok0 + 512].rearrange("(c p) t -> p c t", p=128), wr=[B_xg[i]])
            tt("pool", sq[:], xg[i][:], xg[i][:], ALU.mult, rd=[B_xg[i]], wr=[B_sq])
            for kc in range(KC):
                mm(st_ps[i][:, :], ones_bf[:], sq[:, kc, :], kc == 0, kc == KC - 1, rd=[B_sq, B_const], wb=B_st[i], first=(kc == 0), sig=(kc == KC - 1))
            act(rstd[i][:], st_ps[i][:, :], AF.Sqrt, rd=[B_st[i], B_const], wr=[B_rstd[i]], scale=1.0 / D, bias=eps_t[:])
            recip(rstd[i][:], rstd[i][:], rd=[B_rstd[i]], wr=[B_rstd[i]])
            for kc in range(KC):
                stt("dve", xg[i][:, kc, :], xg[i][:, kc, :], fg_s[:, kc:kc + 1], rstd[i][:], ALU.mult, ALU.mult,
                    rd=[B_rstd[i], B_const, B_sq], wr=[B_xg[i]] if kc == 0 else (), wr_more=() if kc == 0 else [B_xg[i]])
            dma("sp", yout[:, gi * 512:(gi + 1) * 512].rearrange("(c p) t -> p c t", p=128), xg[i][:], rd=[B_xg[i]])
        p.barrier()

    p.emit(nc)
    es.close()
    return nc


def prep_weights(L, norm1_g, norm2_g, w_ada, b_ada, w_in, out_norm_a, out_norm_b, w_out, na_rpb,
                 w_gate, w_up, w_down, final_g, t5_table):
    f = np.float32
    A = lambda a: np.ascontiguousarray(np.asarray(a, dtype=f))
    w = {}
    w["w_in_r"] = A(np.asarray(w_in)[:L].reshape(L, KC, 128, 12, 512).transpose(0, 3, 2, 1, 4))
    w["w_out_r"] = A(np.asarray(w_out)[:L].reshape(L, KC, 128, 4, 512).transpose(0, 3, 2, 1, 4))
    g_ = np.asarray(w_gate)[:L].reshape(L, KC, 128, FC, 128).transpose(0, 3, 2, 1, 4)
    u_ = np.asarray(w_up)[:L].reshape(L, KC, 128, FC, 128).transpose(0, 3, 2, 1, 4)
    w["w_gu_r"] = A(np.stack([g_, u_], axis=3))
    w["w_dn_r"] = A(np.asarray(w_down)[:L].reshape(L, 2, FH, 128, KC, 128).transpose(0, 1, 4, 3, 2, 5))
    w["w_ada_r"] = A(np.asarray(w_ada)[:L].reshape(L, KC, 128, 12, 8, 128).transpose(0, 3, 2, 4, 1, 5))
    w["b_ada_r"] = A(np.asarray(b_ada)[:L].reshape(L, 96, 128).transpose(0, 2, 1))
    w["n1g_r"] = A(np.asarray(norm1_g)[:L].reshape(L, KC, 128).transpose(0, 2, 1))
    w["n2g_r"] = A(np.asarray(norm2_g)[:L].reshape(L, KC, 128).transpose(0, 2, 1))
    ong = np.concatenate([np.asarray(out_norm_a)[:L], np.asarray(out_norm_b)[:L]], axis=1)
    w["ong_r"] = A(ong.reshape(L, KC, 128).transpose(0, 2, 1))
    w["fg_r"] = A(np.asarray(final_g).reshape(KC, 128).T)
    w["t5"] = A(t5_table)
    rp = np.asarray(na_rpb)[:L]
    w["rpbT"] = A(rp[:, :, ::-1, :].transpose(0, 3, 1, 2).reshape(L, 31, 120))
    w.update(static_tables())
    return w


def run_chunks(L, CH, chunks, weights, debug=False):
    W0 = CH + 2 * HT * L
    halo = HT * L * 128
    nc = build(L, CH, debug)
    in_maps = []
    for ci in range(8):
        m = dict(weights)
        xw = np.zeros((W0 * 128, D), np.float32)
        valid = np.zeros((W0 * 128,), np.float32)
        flags = np.zeros((128, 2), np.float32)
        cvec = np.zeros((D,), np.float32)
        if ci < len(chunks):
            xseq, c, a = chunks[ci]
            T = xseq.shape[0]
            lo, hi = a - halo, a + CH * 128 + halo
            s0, s1 = max(lo, 0), min(hi, T)
            xw[s0 - lo:s1 - lo] = xseq[s0:s1]
            valid[s0 - lo:s1 - lo] = 1.0
            flags[:, 0] = 1.0 if a == 0 else 0.0
            flags[:, 1] = 1.0 if a + CH * 128 == T else 0.0
            cvec = np.asarray(c, np.float32)
        m["xin"] = np.ascontiguousarray(xw.T)
        m["valid"] = np.ascontiguousarray(valid.reshape(W0, 128).T)
        m["flags"] = flags
        m["cvec"] = np.ascontiguousarray(cvec.reshape(KC, 128).T)
        in_maps.append(m)
    res = run_bass_kernel_spmd(nc, in_maps, core_ids=list(range(8)))
    if debug:
        return res.results
    return [np.ascontiguousarray(res.results[ci]["yout"].T) for ci in range(len(chunks))]


def kernel(x_prompt, x_sample, c_prompt, c_sample, t5_table, norm1_g, norm2_g, w_ada, b_ada,
           w_in, out_norm_a, out_norm_b, w_out, na_rpb, w_gate, w_up, w_down, final_g):
    L, CH = 4, 32
    x_prompt = np.asarray(x_prompt, np.float32)
    x_sample = np.asarray(x_sample, np.float32)
    c_prompt = np.asarray(c_prompt, np.float32)
    c_sample = np.asarray(c_sample, np.float32)
    weights = prep_weights(L, norm1_g, norm2_g, w_ada, b_ada, w_in, out_norm_a, out_norm_b, w_out, na_rpb,
                           w_gate, w_up, w_down, final_g, t5_table)
    chunks = [(x_prompt[0], c_prompt[0], i * CH * 128) for i in range(4)]
    chunks += [(x_sample[0], c_sample[0], 0), (x_sample[1], c_sample[1], 0)]
    outs = run_chunks(L, CH, chunks, weights)
    y_prompt = np.concatenate(outs[0:4], axis=0)[None].astype(np.float32)
    y_sample = np.stack([outs[4], outs[5]], axis=0).astype(np.float32)
    return (y_prompt, y_sample)
```

```python
import math
import contextlib
import numpy as np
import concourse.bass as bass
import concourse.mybir as mybir
from concourse.bass_utils import run_bass_kernel_spmd

F32 = mybir.dt.float32
BF16 = mybir.dt.bfloat16
AF = mybir.ActivationFunctionType
ALU = mybir.AluOpType

D = 2048
KC = 16
DFF = 5632
FC = 44
FH = 22
NH = 16
HD = 128
QK_SCALE = HD ** -0.5
EPS = 1e-6
HT = 8
BT = 8
GA_W = 2304
GB_W = 1920
N_BUCKETS = 32
MAX_DISTANCE = 1024

ENGINES = ["pe", "act", "dve", "pool", "sp"]
NDSEM = 8


class Buf:
    __slots__ = ("name", "w", "r")

    def __init__(self, name):
        self.name = name
        self.w = None
        self.r = []


class Prog:
    def __init__(self):
        self.ops = {e: [] for e in ENGINES}
        self.cnt = {}
        self.seen = {e: {} for e in ENGINES}
        self.dma_i = {"sp": 0, "pool": 0}
        self.pending = {e: False for e in ENGINES}
        for e in ENGINES:
            self.cnt["c_" + e] = 0
        for q in ("sp", "pool"):
            for i in range(NDSEM):
                self.cnt[f"d_{q}{i}"] = 0

    def _need(self, eng, ev, waits):
        if ev is None:
            return
        k, v = ev
        if self.seen[eng].get(k, 0) >= v:
            return
        self.seen[eng][k] = v
        waits[k] = max(waits.get(k, 0), v)

    def op(self, eng, fn, rd=(), wr=(), wr_more=(), dma=False, sig=True):
        waits = {}
        for b in rd:
            self._need(eng, b.w, waits)
        for b in wr:
            self._need(eng, b.w, waits)
            for ev in b.r:
                self._need(eng, ev, waits)
        if dma:
            i = self.dma_i[eng]
            self.dma_i[eng] = i + 1
            k = f"d_{eng}{i % NDSEM}"
            if self.cnt[k] > 0:
                self._need(eng, (k, self.cnt[k]), waits)
            self.cnt[k] += 16
            inc = (k, 16)
            ev = (k, self.cnt[k])
        else:
            k = "c_" + eng
            if sig:
                self.cnt[k] += 1
                inc = (k, 1)
                ev = (k, self.cnt[k])
                self.pending[eng] = False
            else:
                inc = None
                ev = (k, self.cnt[k] + 1)
                self.pending[eng] = True
        for b in rd:
            b.r.append(ev)
        for b in wr:
            b.w = ev
            b.r = []
        for b in wr_more:
            b.w = ev
        self.ops[eng].append((sorted(waits.items()), fn, inc))
        return ev

    def barrier(self):
        for e in ENGINES:
            assert not self.pending[e], e
        for e in ENGINES:
            waits = {}
            for k, v in self.cnt.items():
                if v > 0:
                    self._need(e, (k, v), waits)
            if waits:
                self.ops[e].append((sorted(waits.items()), None, None))

    def emit(self, nc):
        with contextlib.ExitStack() as st:
            sems = {k: st.enter_context(nc.semaphore(k)) for k in self.cnt}
            block = st.enter_context(nc.Block())

            def run(name):
                def body(eng):
                    for waits, fn, inc in self.ops[name]:
                        for k, v in waits:
                            eng.wait_ge(sems[k], v)
                        if fn is not None:
                            ins = fn(eng)
                            if inc is not None:
                                ins.then_inc(sems[inc[0]], inc[1])
                return body

            block.tensor(run("pe"))
            block.scalar(run("act"))
            block.vector(run("dve"))
            block.gpsimd(run("pool"))
            block.sync(run("sp"))


def t5_bucket(rel):
    nb = N_BUCKETS // 2
    exact = nb // 2
    n = np.abs(rel)
    sign = np.where(rel > 0, nb, 0)
    large = exact + (np.log(np.maximum(n, 1) / exact) / math.log(MAX_DISTANCE / exact) * (nb - exact)).astype(np.int64)
    large = np.minimum(large, nb - 1)
    return (sign + np.where(n < exact, n, large)).astype(np.int32)


def static_tables():
    u = np.arange(GA_W)
    o = 1151 - u
    ok = u <= 2302
    b = t5_bucket(o)
    ohA = np.zeros((32, GA_W), np.float32)
    ohA[b[ok], u[ok]] = 1.0
    ao = np.abs(o)
    mult = ((ao <= 64).astype(np.float32) + ((o % 4 == 0) & (ao <= 256)) + ((o % 16 == 0) & (ao <= 1024))).astype(np.float32)
    mult[~ok] = 0.0
    multA = np.tile(mult[None], (128, 1)).astype(np.float32)
    ub = np.arange(128)
    ohB = np.zeros((31, 128), np.float32)
    cidx = np.clip(63 - ub, -15, 15) + 15
    ohB[cidx[:127], ub[:127]] = 1.0
    qc = np.arange(64)
    cstart = np.clip(qc - 8, 0, 48)
    kcol = np.arange(64)
    cm = (kcol[:, None] >= cstart[None, :]) & (kcol[:, None] < cstart[None, :] + 16)
    cmaskT = np.concatenate([cm, cm], axis=0).astype(np.float32)
    ident = np.eye(128, dtype=np.float32)
    return dict(ohA=ohA, multA=multA, ohB=ohB, cmaskT=cmaskT, ident=ident)


def build(L, CH, debug=False):
    W0 = CH + 2 * HT * L
    WT = W0 * 128
    nc = bass.Bass("TRN2", target_bir_lowering=False)

    def din(name, shape, dtype=F32):
        return nc.dram_tensor(name, list(shape), dtype, kind="ExternalInput")

    def dscr(name, shape, dtype):
        return nc.dram_tensor(name, list(shape), dtype, kind="ExternalOutput" if debug else "Internal")

    xin = din("xin", [D, WT]).ap()
    valid_d = din("valid", [128, W0]).ap()
    flags_d = din("flags", [128, 2]).ap()
    cvec_d = din("cvec", [128, KC]).ap()
    w_in_d = din("w_in_r", [L, 12, 128, KC, 512]).ap()
    w_out_d = din("w_out_r", [L, 4, 128, KC, 512]).ap()
    w_gu_d = din("w_gu_r", [L, FC, 128, 2, KC, 128]).ap()
    w_dn_d = din("w_dn_r", [L, 2, KC, 128, FH, 128]).ap()
    w_ada_d = din("w_ada_r", [L, 12, 128, 8, KC, 128]).ap()
    b_ada_d = din("b_ada_r", [L, 128, 96]).ap()
    n1g_d = din("n1g_r", [L, 128, KC]).ap()
    n2g_d = din("n2g_r", [L, 128, KC]).ap()
    ong_d = din("ong_r", [L, 128, KC]).ap()
    fg_d = din("fg_r", [128, KC]).ap()
    t5_d = din("t5", [32, 8]).ap()
    rpbT_d = din("rpbT", [L, 31, 120]).ap()
    ohA_d = din("ohA", [32, GA_W]).ap()
    multA_d = din("multA", [128, GA_W]).ap()
    ohB_d = din("ohB", [31, 128]).ap()
    cmask_d = din("cmaskT", [128, 64]).ap()
    ident_d = din("ident", [128, 128]).ap()
    yout = nc.dram_tensor("yout", [D, CH * 128], F32, kind="ExternalOutput").ap()

    xs = dscr("xs", [D, WT], F32).ap()
    qTd = dscr("qTd", [NH, 128, WT], BF16).ap()
    kTd = dscr("kTd", [NH, 128, WT], BF16).ap()
    Vd = dscr("Vd", [NH, 128, W0, 130], BF16).ap()
    GdA_t = dscr("GdA", [8, 128, GA_W], BF16)
    GdA = GdA_t.ap()
    GdB_t = dscr("GdB", [8, 128, GB_W], BF16)
    GdB = GdB_t.ap()
    CCd = dscr("CCd", [L, 8, 128, 2 * 896], BF16).ap()

    p = Prog()
    es = contextlib.ExitStack()

    uid = [0]

    def sb(stack, name, shape, dtype):
        uid[0] += 1
        return stack.enter_context(nc.sbuf_tensor(f"s{uid[0]}_{name}", list(shape), dtype))

    def ps(stack, name, shape, dtype):
        uid[0] += 1
        return stack.enter_context(nc.psum_tensor(f"p{uid[0]}_{name}", list(shape), dtype))

    def dma(q, out, in_, rd=(), wr=()):
        p.op(q, lambda e: e.dma_start(out=out, in_=in_), rd=rd, wr=wr, dma=True)

    def mm(out, lhsT, rhs, start, stop, rd, wb, first, sig):
        p.op("pe", lambda e: e.matmul(out, lhsT=lhsT, rhs=rhs, start=start, stop=stop),
             rd=rd, wr=[wb] if first else (), wr_more=() if first else [wb], sig=sig)

    def act(out, in_, func, rd, wr, scale=None, bias=None, wr_more=()):
        kw = {}
        if scale is not None:
            kw["scale"] = scale
        if bias is not None:
            kw["bias"] = bias
        p.op("act", lambda e: e.activation(out=out, in_=in_, func=func, **kw), rd=rd, wr=wr, wr_more=wr_more)

    def actmul(out, in_, m, rd, wr, wr_more=()):
        p.op("act", lambda e: e.mul(out=out, in_=in_, mul=m), rd=rd, wr=wr, wr_more=wr_more)

    def actcopy(out, in_, rd, wr, wr_more=()):
        p.op("act", lambda e: e.copy(out=out, in_=in_), rd=rd, wr=wr, wr_more=wr_more)

    def tt(eng, out, in0, in1, op, rd, wr, wr_more=()):
        p.op(eng, lambda e: e.tensor_tensor(out=out, in0=in0, in1=in1, op=op), rd=rd, wr=wr, wr_more=wr_more)

    def ts(eng, out, in0, s1, op0, rd, wr, s2=None, op1=None, wr_more=()):
        if op1 is None:
            p.op(eng, lambda e: e.tensor_scalar(out=out, in0=in0, scalar1=s1, scalar2=None, op0=op0),
                 rd=rd, wr=wr, wr_more=wr_more)
        else:
            p.op(eng, lambda e: e.tensor_scalar(out=out, in0=in0, scalar1=s1, scalar2=s2, op0=op0, op1=op1),
                 rd=rd, wr=wr, wr_more=wr_more)

    def stt(eng, out, in0, scalar, in1, op0, op1, rd, wr, wr_more=()):
        p.op(eng, lambda e: e.scalar_tensor_tensor(out=out, in0=in0, scalar=scalar, in1=in1, op0=op0, op1=op1),
             rd=rd, wr=wr, wr_more=wr_more)

    def recip(out, in_, rd, wr):
        p.op("dve", lambda e: e.reciprocal(out=out, in_=in_), rd=rd, wr=wr)

    def copy(eng, out, in_, rd, wr, wr_more=()):
        p.op(eng, lambda e: e.tensor_copy(out=out, in_=in_), rd=rd, wr=wr, wr_more=wr_more)

    def memset(eng, ap, val, wr):
        p.op(eng, lambda e: e.memset(ap, val), wr=wr)

    def transpose(out, in_, ident, rd, wb):
        p.op("pe", lambda e: e.transpose(out, in_, ident), rd=rd, wr=[wb])

    ones_bf = sb(es, "ones_bf", [128, 128], BF16)
    ident_bf = sb(es, "ident_bf", [128, 128], BF16)
    eps_t = sb(es, "eps_t", [128, 1], F32)
    tiny_t = sb(es, "tiny_t", [128, 1], F32)
    valid_s = sb(es, "valid_s", [128, W0], F32)
    flags_s = sb(es, "flags_s", [128, 2], F32)
    modc = sb(es, "modc", [128, L, 96], F32)
    gm = sb(es, "gm", [128, L, 32], F32)
    ong_s = sb(es, "ong_s", [128, L, KC], F32)
    fg_s = sb(es, "fg_s", [128, KC], F32)
    onescol = sb(es, "onescol", [128, 4, 1], F32)
    B_const = Buf("const")

    mm_ps = [ps(es, f"mm{i}", [128, 512], F32) for i in range(2)]
    mu_ps = [ps(es, f"mu{i}", [128, 512], F32) for i in range(2)]
    st_ps = [ps(es, f"st{i}", [128, 512], F32) for i in range(2)]
    pv_ps = ps(es, "pv", [128, 2, 256], F32)
    tp_ps = ps(es, "tp", [128, 2, 512], BF16)
    B_mm = [Buf(f"mm{i}") for i in range(2)]
    B_mu = [Buf(f"mu{i}") for i in range(2)]
    B_st = [Buf(f"st{i}") for i in range(2)]
    B_pv = [Buf(f"pv{i}") for i in range(2)]
    B_tp = [Buf(f"tp{i}") for i in range(2)]

    actT = sb(es, "actT", [128, KC, BT * 128], BF16)
    B_act = Buf("actT")
    slab = [sb(es, f"slab{i}", [128, KC, 512], BF16) for i in range(2)]
    B_slab = [Buf(f"slab{i}") for i in range(2)]
    slab_i = [0]
    wgu = [sb(es, f"wgu{i}", [128, 2, KC, 128], BF16) for i in range(2)]
    B_wgu = [Buf("wgu0"), Buf("wgu1")]
    pre = {}

    def next_slab():
        i = slab_i[0] % 2
        slab_i[0] += 1
        return slab[i], B_slab[i]

    rot = {}

    def rotate(key, n):
        i = rot.get(key, 0)
        rot[key] = i + 1
        return i % n

    memset("pool", ones_bf[:], 1.0, [B_const])
    memset("pool", eps_t[:], EPS, [B_const])
    memset("pool", tiny_t[:], 1e-30, [B_const])
    memset("pool", onescol[:], 1.0, [B_const])
    dma("pool", ident_bf[:], ident_d[:, :], wr=[B_const])
    dma("sp", valid_s[:], valid_d[:, :], wr=[B_const])
    dma("sp", flags_s[:], flags_d[:, :], wr=[B_const])
    dma("sp", fg_s[:], fg_d[:, :], wr=[B_const])
    for l in range(L):
        dma("sp", ong_s[:, l, :], ong_d[l], wr=[B_const])
    p.barrier()

    with contextlib.ExitStack() as s0:
        cv = sb(s0, "cv", [128, KC], F32)
        csil = sb(s0, "csil", [128, KC], BF16)
        bada = sb(s0, "bada", [128, L, 96], F32)
        n1g = sb(s0, "n1g", [128, L, KC], F32)
        n2g = sb(s0, "n2g", [128, L, KC], F32)
        wa = [sb(s0, f"wa{i}", [128, 8, KC, 128], BF16) for i in range(2)]
        B_wa = [Buf("wa0"), Buf("wa1")]
        B_c0 = Buf("c0")
        dma("sp", cv[:], cvec_d[:, :], wr=[B_c0])
        for l in range(L):
            dma("sp", bada[:, l, :], b_ada_d[l], wr=[B_c0])
            dma("sp", n1g[:, l, :], n1g_d[l], wr=[B_c0])
            dma("sp", n2g[:, l, :], n2g_d[l], wr=[B_c0])
        p.barrier()
        act(csil[:], cv[:], AF.Silu, rd=[B_c0], wr=[B_c0])
        for l in range(L):
            for jg in range(12):
                i = (l * 12 + jg) % 2
                dma("pool", wa[i][:], w_ada_d[l, jg], wr=[B_wa[i]])
                for j in range(8):
                    col = jg * 8 + j
                    for kc in range(KC):
                        mm(st_ps[0][:, col:col + 1], wa[i][:, j, kc, :], csil[:, kc:kc + 1],
                           kc == 0, kc == KC - 1, rd=[B_wa[i], B_c0], wb=B_st[0],
                           first=(jg == 0 and j == 0 and kc == 0), sig=(kc == KC - 1))
            tt("dve", modc[:, l, :], st_ps[0][:, 0:96], bada[:, l, :], ALU.add, rd=[B_st[0], B_c0], wr=[B_const])
            stt("dve", gm[:, l, 0:16], modc[:, l, 16:32], 1.0, n1g[:, l, :], ALU.add, ALU.mult, rd=[B_const, B_c0], wr=[B_const])
            stt("dve", gm[:, l, 16:32], modc[:, l, 64:80], 1.0, n2g[:, l, :], ALU.add, ALU.mult, rd=[B_const, B_c0], wr=[B_const])
        p.barrier()

    with contextlib.ExitStack() as s0:
        t5s = sb(s0, "t5s", [32, 8], F32)
        ohA = sb(s0, "ohA", [32, GA_W], BF16)
        multA = sb(s0, "multA", [128, GA_W], F32)
        ones32 = sb(s0, "ones32", [32, 128], F32)
        lh = [sb(s0, f"lh{i}", [32, 128], BF16) for i in range(2)]
        gA = [sb(s0, f"gA{i}", [128, GA_W], BF16) for i in range(2)]
        etmp = [sb(s0, f"etmp{i}", [128, 512], F32) for i in range(2)]
        B_m = Buf("m")
        B_lh = [Buf("lh0"), Buf("lh1")]
        B_gA = [Buf("gA0"), Buf("gA1")]
        B_et = [Buf("et0"), Buf("et1")]
        dma("sp", t5s[:], t5_d[:, :], wr=[B_m])
        dma("pool", ohA[:], ohA_d[:, :], wr=[B_m])
        dma("sp", multA[:], multA_d[:, :], wr=[B_m])
        memset("pool", ones32[:], 1.0, [B_m])
        p.barrier()
        for h in range(8):
            i = h % 2
            ts("dve", lh[i][:], ones32[:], t5s[:, h:h + 1], ALU.mult, rd=[B_m], wr=[B_lh[i]])
            for c in range(5):
                n = min(512, GA_W - c * 512)
                j = rotate("pA", 2)
                mm(mm_ps[j][:, 0:n], lh[i][:], ohA[:, c * 512:c * 512 + n], True, True, rd=[B_lh[i], B_m], wb=B_mm[j], first=True, sig=True)
                act(etmp[j][:, 0:n], mm_ps[j][:, 0:n], AF.Exp, rd=[B_mm[j]], wr=[B_et[j]])
                tt("dve", gA[i][:, c * 512:c * 512 + n], etmp[j][:, 0:n], multA[:, c * 512:c * 512 + n], ALU.mult,
                   rd=[B_et[j], B_m], wr=[B_gA[i]] if c == 0 else (), wr_more=() if c == 0 else [B_gA[i]])
            dma("sp", GdA[h], gA[i][:], rd=[B_gA[i]])
        p.barrier()

    with contextlib.ExitStack() as s0:
        rpbs = sb(s0, "rpbs", [31, L, 120], F32)
        ohB = sb(s0, "ohB", [31, 128], BF16)
        ones31 = sb(s0, "ones31", [31, 128], F32)
        cmask = sb(s0, "cmask", [128, 64], F32)
        lhb = [sb(s0, f"lhb{i}", [31, 128], BF16) for i in range(2)]
        gB = [sb(s0, f"gB{i}", [128, GB_W], BF16) for i in range(2)]
        ccf = [sb(s0, f"ccf{i}", [128, 14, 64], BF16) for i in range(2)]
        ccm = [sb(s0, f"ccm{i}", [128, 14, 64], BF16) for i in range(2)]
        cc2 = [sb(s0, f"cc2{i}", [128, 2, 14, 64], BF16) for i in range(2)]
        B_m = Buf("mB")
        B_lhb = [Buf("lhb0"), Buf("lhb1")]
        B_gB = [Buf("gB0"), Buf("gB1")]
        B_ccf = [[Buf("ccf00"), Buf("ccf01")], [Buf("ccf10"), Buf("ccf11")]]
        B_ccm = [Buf("ccm0"), Buf("ccm1")]
        B_cc2 = [Buf("cc20"), Buf("cc21")]
        B_GdB = [Buf(f"GdB{h}") for h in range(8)]
        for l in range(L):
            dma("sp", rpbs[:, l, :], rpbT_d[l], wr=[B_m])
        dma("pool", ohB[:], ohB_d[:, :], wr=[B_m])
        dma("sp", cmask[:], cmask_d[:, :], wr=[B_m])
        memset("pool", ones31[:], 1.0, [B_m])
        p.barrier()
        for l in range(L):
            for h in range(8):
                i = (l * 8 + h) % 2
                for sp_ in range(15):
                    jb = rotate("lhb", 2)
                    ts("dve", lhb[jb][:], ones31[:], rpbs[:, l, h * 15 + sp_:h * 15 + sp_ + 1], ALU.mult, rd=[B_m], wr=[B_lhb[jb]])
                    if sp_ % 4 == 0:
                        j = rotate("pA", 2)
                    mm(mm_ps[j][:, (sp_ % 4) * 128:(sp_ % 4 + 1) * 128], lhb[jb][:], ohB[:], True, True,
                       rd=[B_lhb[jb], B_m], wb=B_mm[j], first=(sp_ % 4 == 0), sig=True)
                    if sp_ % 4 == 3 or sp_ == 14:
                        s0_ = (sp_ // 4) * 4
                        n = (sp_ - s0_ + 1) * 128
                        act(gB[i][:, s0_ * 128:s0_ * 128 + n], mm_ps[j][:, 0:n], AF.Exp, rd=[B_mm[j]],
                            wr=[B_gB[i]] if s0_ == 0 else (), wr_more=() if s0_ == 0 else [B_gB[i]])
                dma("sp", GdB[h], gB[i][:], rd=[B_gB[i]], wr=[B_GdB[h]])
                for kr in range(2):
                    src = bass.AP(GdB_t, h * 128 * GB_W + (1 - kr) * 128 + 63, [[GB_W - 1, 64], [128, 14], [1, 64]])
                    dma("sp", ccf[i][kr * 64:(kr + 1) * 64, :, :], src, rd=[B_GdB[h]], wr=[B_ccf[i][kr]])
                for s_ in range(14):
                    tt("dve", ccm[i][:, s_, :], ccf[i][:, s_, :], cmask[:], ALU.mult, rd=[B_ccf[i][0], B_ccf[i][1], B_m],
                       wr=[B_ccm[i]] if s_ == 0 else (), wr_more=() if s_ == 0 else [B_ccm[i]])
                memset("pool", cc2[i][:], 0.0, [B_cc2[i]])
                cps = [(0, 0, 3, 11), (64, 0, 4, 12), (0, 1, 0, 3), (0, 1, 11, 14), (64, 1, 0, 4), (64, 1, 12, 14)]
                for (p0, w, a, b_) in cps:
                    copy("dve", cc2[i][p0:p0 + 64, w, a:b_, :], ccm[i][p0:p0 + 64, a:b_, :], rd=[B_ccm[i], B_cc2[i]], wr=(), wr_more=[B_cc2[i]])
                dma("sp", CCd[l, h], cc2[i][:], rd=[B_cc2[i]])
        p.barrier()

    def norm_block(l, which, src_ap, tok0, xg, B_xg, sq, B_sq, tmpn, B_tmpn, rstd, B_rstd, g, load=True):
        if load:
            dma("pool", xg[:], src_ap[:, tok0:tok0 + 512].rearrange("(c p) t -> p c t", p=128), wr=[B_xg])
        tt("pool", sq[:], xg[:], xg[:], ALU.mult, rd=[B_xg], wr=[B_sq])
        for kc in range(KC):
            mm(st_ps[0][:, :], ones_bf[:], sq[:, kc, :], kc == 0, kc == KC - 1, rd=[B_sq, B_const], wb=B_st[0], first=(kc == 0), sig=(kc == KC - 1))
        act(rstd[:], st_ps[0][:, :], AF.Sqrt, rd=[B_st[0], B_const], wr=[B_rstd], scale=1.0 / D, bias=eps_t[:])
        recip(rstd[:], rstd[:], rd=[B_rstd], wr=[B_rstd])
        goff = 0 if which == 1 else 16
        soff = 0 if which == 1 else 48
        for kc in range(KC):
            j = rotate("tmpn", 2)
            tt("dve", tmpn[j][:], xg[:, kc, :], rstd[:], ALU.mult, rd=[B_xg, B_rstd], wr=[B_tmpn[j]])
            act(actT[:, kc, g * 512:(g + 1) * 512], tmpn[j][:], AF.Identity, rd=[B_tmpn[j], B_const],
                wr=[B_act] if (kc == 0 and g == 0) else (), wr_more=() if (kc == 0 and g == 0) else [B_act],
                scale=gm[:, l, goff + kc:goff + kc + 1], bias=modc[:, l, soff + kc:soff + kc + 1])

    for l in range(L):
        kv0, kv1 = HT * l, W0 - HT * l
        o0, o1 = HT * (l + 1), W0 - HT * (l + 1)
        xsrc = xin if l == 0 else xs

        with contextlib.ExitStack() as s1:
            xg = [sb(s1, f"xg{i}", [128, KC, 512], F32) for i in range(2)]
            B_xg = [Buf("xg0"), Buf("xg1")]
            sq = sb(s1, "sq", [128, KC, 512], BF16)
            B_sq = Buf("sq")
            tmpn = [sb(s1, f"tmpn{i}", [128, 512], F32) for i in range(2)]
            B_tmpn = [Buf("tmpn0"), Buf("tmpn1")]
            rstd = sb(s1, "rstd", [128, 512], F32)
            B_rstd = Buf("rstd")
            kst = [sb(s1, f"kst{i}", [128, 512], BF16) for i in range(3)]
            B_kst = [Buf(f"kst{i}") for i in range(3)]
            vst = [sb(s1, f"vst{i}", [128, 4, BT, 130], BF16) for i in range(2)]
            B_vst = [Buf("vst0"), Buf("vst1")]
            B_vone = [Buf("vone0"), Buf("vone1")]
            for b0 in range(kv0, kv1, BT):
                tok0 = b0 * 128
                need_q = (o0 <= b0 < o1)
                for g in range(2):
                    norm_block(l, 1, xsrc, tok0 + g * 512, xg[g], B_xg[g], sq, B_sq, tmpn, B_tmpn, rstd, B_rstd, g)
                for s in range(12):
                    kind = ("q", "k", "v")[(s % 6) // 2]
                    if kind == "q" and not need_q:
                        continue
                    sl, B_sl = next_slab()
                    dma("pool", sl[:], w_in_d[l, s], wr=[B_sl])
                    hbase = (s // 6) * 8 + (s % 2) * 4
                    if kind in ("q", "k"):
                        dst = qTd if kind == "q" else kTd
                        for cc in range(4):
                            for g in range(2):
                                j = rotate("mm", 2)
                                for kc in range(KC):
                                    mm(mm_ps[j][:, :], sl[:, kc, cc * 128:(cc + 1) * 128], actT[:, kc, g * 512:(g + 1) * 512],
                                       kc == 0, kc == KC - 1, rd=[B_sl, B_act], wb=B_mm[j], first=(kc == 0), sig=(kc == KC - 1))
                                ks = rotate("kst", 3)
                                if kind == "q":
                                    actmul(kst[ks][:], mm_ps[j][:, :], QK_SCALE, rd=[B_mm[j]], wr=[B_kst[ks]])
                                else:
                                    copy("dve", kst[ks][:], mm_ps[j][:, :], rd=[B_mm[j]], wr=[B_kst[ks]])
                                dma("sp", dst[hbase + cc, :, tok0 + g * 512:tok0 + (g + 1) * 512], kst[ks][:], rd=[B_kst[ks]])
                    else:
                        vi = rotate("vst", 2)
                        for t in range(BT):
                            j = rotate("mm", 2)
                            for kc in range(KC):
                                mm(mm_ps[j][:, :], actT[:, kc, t * 128:(t + 1) * 128], sl[:, kc, :],
                                   kc == 0, kc == KC - 1, rd=[B_sl, B_act], wb=B_mm[j], first=(kc == 0), sig=(kc == KC - 1))
                            vcol = valid_s[:, b0 + t:b0 + t + 1]
                            ts("dve", vst[vi][:, :, t, 0:128], mm_ps[j][:, :].rearrange("p (h d) -> p h d", h=4), vcol, ALU.mult,
                               rd=[B_mm[j], B_const], wr=[B_vst[vi]] if t == 0 else (), wr_more=() if t == 0 else [B_vst[vi]])
                            ts("pool", vst[vi][:, :, t, 128:129], onescol[:], vcol, ALU.mult, rd=[B_const],
                               wr=[B_vone[vi]] if t == 0 else (), wr_more=() if t == 0 else [B_vone[vi]])
                        for hh in range(4):
                            dma("sp", Vd[hbase + hh, :, b0:b0 + BT, :], vst[vi][:, hh, :, :], rd=[B_vst[vi], B_vone[vi]])
            p.barrier()

        for q0 in range(o0, o1, BT):
            tok0 = q0 * 128
            spec = {}
            c0 = HT * L
            spec[c0] = ("s0", 0)
            spec[c0 + 1] = ("s1", 0)
            spec[c0 + CH - 2] = ("e1", 1)
            spec[c0 + CH - 1] = ("e0", 1)
            with contextlib.ExitStack() as s3:
                kwin = [sb(s3, f"kwin{i}", [128, 24 * 128], BF16) for i in range(2)]
                vwin = [sb(s3, f"vwin{i}", [128, 24, 130], BF16) for i in range(2)]
                qwin = [sb(s3, f"qwin{i}", [128, BT * 128], BF16) for i in range(2)]
                em = [sb(s3, f"em{i}", [128, 2176], BF16) for i in range(2)]
                mb = [sb(s3, f"mb{i}", [128, 896], BF16) for i in range(2)]
                Pb = [sb(s3, f"Pb{i}", [128, 512], BF16) for i in range(3)]
                Pm = [sb(s3, f"Pm{i}", [128, 512], BF16) for i in range(4)]
                yst = [sb(s3, f"yst{i}", [128, 128], BF16) for i in range(2)]
                rden = [sb(s3, f"rden{i}", [128, 1], F32) for i in range(2)]
                B_kw = [Buf("kw0"), Buf("kw1")]
                B_vw = [Buf("vw0"), Buf("vw1")]
                B_qw = [Buf("qw0"), Buf("qw1")]
                B_em = [Buf("em0"), Buf("em1")]
                B_mb = [Buf("mb0"), Buf("mb1")]
                B_Pb = [Buf("Pb0"), Buf("Pb1"), Buf("Pb2")]
                B_Pm = [Buf("Pm0"), Buf("Pm1"), Buf("Pm2"), Buf("Pm3")]
                B_yst = [Buf("yst0"), Buf("yst1")]
                B_rd = [Buf("rd0"), Buf("rd1")]
                first_act = [True]
                for s_ in range(2):
                    sl_, B_sl_ = next_slab()
                    dma("pool", sl_[:], w_out_d[l, s_], wr=[B_sl_])
                    pre[("wo", s_)] = (sl_, B_sl_)
                pend = []
                LA = 2
                S_ps = [mm_ps[0], mm_ps[1], mu_ps[0], mu_ps[1]]
                B_S = [B_mm[0], B_mm[1], B_mu[0], B_mu[1]]

                def emit_pv(tk):
                    (bi_, hd_, t_, hk_, gp_, pm_, vb_, done0, ntot_) = tk
                    done_ = done0
                    for i_, dl in enumerate(gp_):
                        kw_ = t_ + hk_ + dl
                        mm(pv_ps[:, vb_, 0:129], Pm[pm_][:, i_ * 128:(i_ + 1) * 128], vwin[bi_][:, kw_, 0:129],
                           done_ == 0, done_ == ntot_ - 1, rd=[B_Pm[pm_], B_vw[bi_]], wb=B_pv[vb_],
                           first=(done_ == 0), sig=(done_ == ntot_ - 1))
                        done_ += 1
                    if done_ == ntot_:
                        ri = rotate("rden", 2)
                        ts("dve", rden[ri][:], pv_ps[:, vb_, 128:129], tiny_t[:, 0:1], ALU.max, rd=[B_pv[vb_], B_const], wr=[B_rd[ri]])
                        recip(rden[ri][:], rden[ri][:], rd=[B_rd[ri]], wr=[B_rd[ri]])
                        yi = rotate("yst", 2)
                        ts("dve", yst[yi][:], pv_ps[:, vb_, 0:128], rden[ri][:, 0:1], ALU.mult, rd=[B_pv[vb_], B_rd[ri]], wr=[B_yst[yi]])
                        ti = rotate("tp", 2)
                        transpose(tp_ps[:, ti, 0:128], yst[yi][:], ident_bf[:], rd=[B_yst[yi], B_const], wb=B_tp[ti])
                        actcopy(actT[:, hd_, t_ * 128:(t_ + 1) * 128], tp_ps[:, ti, 0:128], rd=[B_tp[ti]],
                                wr=[B_act] if first_act[0] else (), wr_more=() if first_act[0] else [B_act])
                        first_act[0] = False

                for hd in range(NH):
                    isA = hd < 8
                    hk = 8 if isA else 3
                    nw = BT + 2 * hk
                    bi = hd % 2
                    dma("sp", kwin[bi][:, 0:nw * 128], kTd[hd, :, (q0 - hk) * 128:(q0 + BT + hk) * 128], wr=[B_kw[bi]])
                    dma("sp", vwin[bi][:, 0:nw, :], Vd[hd, :, q0 - hk:q0 + BT + hk, :], wr=[B_vw[bi]])
                    dma("sp", qwin[bi][:], qTd[hd, :, tok0:tok0 + BT * 128], wr=[B_qw[bi]])
                    if isA:
                        src = bass.AP(GdA_t, hd * 128 * GA_W + 127, [[GA_W - 1, 128], [1, 2176]])
                        dma("sp", em[bi][:], src, wr=[B_em[bi]])
                    else:
                        dma("sp", em[bi][:, 0:1792], CCd[l, hd - 8], wr=[B_em[bi]])
                    mb_ready = {}
                    for t in range(BT):
                        qt = q0 + t
                        if isA:
                            groups = [[8, 7, 6, 5], [4, 3, 2, 1], [0, -1, -2, -3], [-4, -5, -6, -7], [-8]]
                            msrc = lambda dmax, n: (em[bi][:, 128 * (8 - dmax):128 * (8 - dmax) + 128 * n], B_em[bi])
                        else:
                            kind = spec.get(qt)
                            if kind is None:
                                groups = [[2, 1, 0, -1], [-2]]
                                msrc = lambda dmax, n: (em[bi][:, (6 - 2 * dmax) * 64:(6 - 2 * dmax) * 64 + 128 * n], B_em[bi])
                            else:
                                nm, fl = kind
                                if fl not in mb_ready:
                                    mi = rotate("mb", 2)
                                    stt("dve", mb[mi][:], em[bi][:, 896:1792], flags_s[:, fl:fl + 1], em[bi][:, 0:896],
                                        ALU.mult, ALU.add, rd=[B_em[bi], B_const], wr=[B_mb[mi]])
                                    mb_ready[fl] = mi
                                mi = mb_ready[fl]
                                groups = {"s0": [[3, 2, 1, 0], [-1, -2]], "s1": [[2, 1, 0, -1], [-2]],
                                          "e1": [[2, 1, 0, -1], [-2]], "e0": [[2, 1, 0, -1], [-2, -3]]}[nm]
                                msrc = (lambda mi_: (lambda dmax, n: (mb[mi_][:, (6 - 2 * dmax) * 64:(6 - 2 * dmax) * 64 + 128 * n], B_mb[mi_])))(mi)
                        vb = rotate("pv", 2)
                        ntot = sum(len(gp) for gp in groups)
                        done = 0
                        for gp in groups:
                            n = len(gp)
                            sbk = rotate("Sbank", 4)
                            for i_, dl in enumerate(gp):
                                kw_ = t + hk + dl
                                mm(S_ps[sbk][:, i_ * 128:(i_ + 1) * 128], kwin[bi][:, kw_ * 128:(kw_ + 1) * 128],
                                   qwin[bi][:, t * 128:(t + 1) * 128], True, True, rd=[B_kw[bi], B_qw[bi]], wb=B_S[sbk],
                                   first=(i_ == 0), sig=(i_ == n - 1))
                            pb = rotate("Pb", 3)
                            act(Pb[pb][:, 0:n * 128], S_ps[sbk][:, 0:n * 128], AF.Exp, rd=[B_S[sbk]], wr=[B_Pb[pb]])
                            pm = rotate("Pm", 4)
                            map_, B_map = msrc(gp[0], n)
                            tt("dve", Pm[pm][:, 0:n * 128], Pb[pb][:, 0:n * 128], map_, ALU.mult, rd=[B_Pb[pb], B_map], wr=[B_Pm[pm]])
                            pend.append((bi, hd, t, hk, gp, pm, vb, done, ntot))
                            done += n
                            if len(pend) > LA:
                                emit_pv(pend.pop(0))
                while pend:
                    emit_pv(pend.pop(0))
                p.barrier()

            with contextlib.ExitStack() as s4:
                for ffc_ in range(2):
                    wi_ = rotate("wgu", 2)
                    dma("pool", wgu[wi_][:], w_gu_d[l, ffc_], wr=[B_wgu[wi_]])
                    pre[("gu", ffc_)] = wi_
                xnew = sb(s4, "xnew", [128, 2, KC, 512], F32)
                B_xn = [Buf("xn0"), Buf("xn1")]
                sqy = sb(s4, "sqy", [128, KC, 512], BF16)
                B_sqy = Buf("sqy")
                rs = [sb(s4, f"rs{i}", [128, 512], F32) for i in range(2)]
                B_rs = [Buf("rs0"), Buf("rs1")]
                xp = [sb(s4, f"xp{i}", [128, 512], F32) for i in range(3)]
                B_xp = [Buf(f"xp{i}") for i in range(3)]
                sqp = [sb(s4, f"sqp{i}", [128, 512], BF16) for i in range(2)]
                B_sqp = [Buf("sqp0"), Buf("sqp1")]
                tmpn = [sb(s4, f"tmq{i}", [128, 512], F32) for i in range(2)]
                B_tmpn = [Buf("tmq0"), Buf("tmq1")]
                rstd2 = [sb(s4, f"rstd2{i}", [128, 512], F32) for i in range(2)]
                B_rstd2 = [Buf("r20"), Buf("r21")]
                for g in range(2):
                    tt("pool", sqy[:], actT[:, :, g * 512:(g + 1) * 512], actT[:, :, g * 512:(g + 1) * 512], ALU.mult, rd=[B_act], wr=[B_sqy])
                    for half in range(2):
                        for kk in range(8):
                            kc = half * 8 + kk
                            mm(st_ps[half][:, :], ones_bf[:], sqy[:, kc, :], kk == 0, kk == 7, rd=[B_sqy, B_const], wb=B_st[half], first=(kk == 0), sig=(kk == 7))
                        act(rs[half][:], st_ps[half][:, :], AF.Sqrt, rd=[B_st[half], B_const], wr=[B_rs[half]], scale=1.0 / 1024, bias=eps_t[:])
                        recip(rs[half][:], rs[half][:], rd=[B_rs[half]], wr=[B_rs[half]])
                    for kc in range(KC):
                        half = kc // 8
                        stt("dve", actT[:, kc, g * 512:(g + 1) * 512], actT[:, kc, g * 512:(g + 1) * 512], ong_s[:, l, kc:kc + 1], rs[half][:],
                            ALU.mult, ALU.mult, rd=[B_rs[half], B_const, B_act], wr=[B_act] if kc == 0 else (), wr_more=() if kc == 0 else [B_act])
                for s in range(4):
                    if ("wo", s) in pre:
                        sl, B_sl = pre.pop(("wo", s))
                    else:
                        sl, B_sl = next_slab()
                        dma("pool", sl[:], w_out_d[l, s], wr=[B_sl])
                    for cc in range(4):
                        ch = s * 4 + cc
                        for g in range(2):
                            j = rotate("mm", 2)
                            for kc in range(KC):
                                mm(mm_ps[j][:, :], sl[:, kc, cc * 128:(cc + 1) * 128], actT[:, kc, g * 512:(g + 1) * 512],
                                   kc == 0, kc == KC - 1, rd=[B_sl, B_act], wb=B_mm[j], first=(kc == 0), sig=(kc == KC - 1))
                            xi = rotate("xp", 3)
                            dma("sp", xp[xi][:], xsrc[ch * 128:(ch + 1) * 128, tok0 + g * 512:tok0 + (g + 1) * 512], wr=[B_xp[xi]])
                            stt("dve", xnew[:, g, ch, :], mm_ps[j][:, :], modc[:, l, 32 + ch:33 + ch], xp[xi][:], ALU.mult, ALU.add,
                                rd=[B_mm[j], B_xp[xi], B_const], wr=[B_xn[g]] if ch == 0 else (), wr_more=() if ch == 0 else [B_xn[g]])
                            dma("sp", xs[ch * 128:(ch + 1) * 128, tok0 + g * 512:tok0 + (g + 1) * 512], xnew[:, g, ch, :], rd=[B_xn[g]])
                            si = rotate("sqp", 2)
                            tt("pool", sqp[si][:], xnew[:, g, ch, :], xnew[:, g, ch, :], ALU.mult, rd=[B_xn[g]], wr=[B_sqp[si]])
                            mm(st_ps[g][:, :], ones_bf[:], sqp[si][:], ch == 0, ch == KC - 1, rd=[B_sqp[si], B_const], wb=B_st[g], first=(ch == 0), sig=True)
                for g in range(2):
                    act(rstd2[g][:], st_ps[g][:, :], AF.Sqrt, rd=[B_st[g], B_const], wr=[B_rstd2[g]], scale=1.0 / D, bias=eps_t[:])
                    recip(rstd2[g][:], rstd2[g][:], rd=[B_rstd2[g]], wr=[B_rstd2[g]])
                    for kc in range(KC):
                        j = rotate("tmq", 2)
                        tt("dve", tmpn[j][:], xnew[:, g, kc, :], rstd2[g][:], ALU.mult, rd=[B_xn[g], B_rstd2[g]], wr=[B_tmpn[j]])
                        act(actT[:, kc, g * 512:(g + 1) * 512], tmpn[j][:], AF.Identity, rd=[B_tmpn[j], B_const],
                            wr=[B_act] if (kc == 0 and g == 0) else (), wr_more=() if (kc == 0 and g == 0) else [B_act],
                            scale=gm[:, l, 16 + kc:17 + kc], bias=modc[:, l, 48 + kc:49 + kc])
                p.barrier()

            with contextlib.ExitStack() as s6:
                aT = sb(s6, "aT", [128, FH, BT * 128], BF16)
                B_aT = Buf("aT")
                wd = [sb(s6, f"wd{i}", [128, FH, 128], BF16) for i in range(2)]
                B_wd = [Buf("wd0"), Buf("wd1")]
                sg = [sb(s6, f"sg{i}", [128, 512], F32) for i in range(2)]
                B_sg = [Buf("sg0"), Buf("sg1")]
                xp = [sb(s6, f"xq{i}", [128, 512], F32) for i in range(3)]
                B_xp = [Buf(f"xq{i}") for i in range(3)]
                xo = [sb(s6, f"xo{i}", [128, 512], F32) for i in range(3)]
                B_xo = [Buf(f"xo{i}") for i in range(3)]
                B_piece = [[Buf(f"pc{c}_{g}") for g in range(2)] for c in range(KC)]
                for fh in range(2):
                    for fi in range(FH):
                        ffc = fh * FH + fi
                        if ("gu", ffc) in pre:
                            wi = pre.pop(("gu", ffc))
                        else:
                            wi = rotate("wgu", 2)
                            dma("pool", wgu[wi][:], w_gu_d[l, ffc], wr=[B_wgu[wi]])
                        for g in range(2):
                            j = rotate("mm", 2)
                            for kc in range(KC):
                                mm(mm_ps[j][:, :], wgu[wi][:, 0, kc, :], actT[:, kc, g * 512:(g + 1) * 512], kc == 0, kc == KC - 1,
                                   rd=[B_wgu[wi], B_act], wb=B_mm[j], first=(kc == 0), sig=(kc == KC - 1))
                            ju = rotate("mu", 2)
                            for kc in range(KC):
                                mm(mu_ps[ju][:, :], wgu[wi][:, 1, kc, :], actT[:, kc, g * 512:(g + 1) * 512], kc == 0, kc == KC - 1,
                                   rd=[B_wgu[wi], B_act], wb=B_mu[ju], first=(kc == 0), sig=(kc == KC - 1))
                            gi = rotate("sg", 2)
                            act(sg[gi][:], mm_ps[j][:, :], AF.Silu, rd=[B_mm[j]], wr=[B_sg[gi]])
                            tt("dve", aT[:, fi, g * 512:(g + 1) * 512], sg[gi][:], mu_ps[ju][:, :], ALU.mult, rd=[B_sg[gi], B_mu[ju]],
                               wr=[B_aT] if (fi == 0 and g == 0) else (), wr_more=() if (fi == 0 and g == 0) else [B_aT])
                    for ch in range(KC):
                        wi = rotate("wd", 2)
                        dma("pool", wd[wi][:], w_dn_d[l, fh, ch], wr=[B_wd[wi]])
                        for g in range(2):
                            j = rotate("mm", 2)
                            for fi in range(FH):
                                mm(mm_ps[j][:, :], wd[wi][:, fi, :], aT[:, fi, g * 512:(g + 1) * 512], fi == 0, fi == FH - 1,
                                   rd=[B_wd[wi], B_aT], wb=B_mm[j], first=(fi == 0), sig=(fi == FH - 1))
                            xi = rotate("xq", 3)
                            xsl = xs[ch * 128:(ch + 1) * 128, tok0 + g * 512:tok0 + (g + 1) * 512]
                            dma("sp", xp[xi][:], xsl, rd=[B_piece[ch][g]], wr=[B_xp[xi]])
                            oi = rotate("xo", 3)
                            stt("dve", xo[oi][:], mm_ps[j][:, :], modc[:, l, 80 + ch:81 + ch], xp[xi][:], ALU.mult, ALU.add,
                                rd=[B_mm[j], B_xp[xi], B_const], wr=[B_xo[oi]])
                            dma("sp", xsl, xo[oi][:], rd=[B_xo[oi]], wr=[B_piece[ch][g]])
                p.barrier()

    with contextlib.ExitStack() as s8:
        xg = [sb(s8, f"fxg{i}", [128, KC, 512], F32) for i in range(2)]
        B_xg = [Buf("fxg0"), Buf("fxg1")]
        sq = sb(s8, "fsq", [128, KC, 512], BF16)
        B_sq = Buf("fsq")
        rstd = [sb(s8, f"frstd{i}", [128, 512], F32) for i in range(2)]
        B_rstd = [Buf("fr0"), Buf("fr1")]
        c0 = HT * L
        for gi in range(CH // 4):
            tok0 = (c0 + gi * 4) * 128
            i = gi % 2
            dma("sp", xg[i][:], xs[:, tok0:tok0 + 512].rearrange("(c p) t -> p c t", p=128), wr=[B_xg[i]])
            tt("pool", sq[:], xg[i][:], xg[i][:], ALU.mult, rd=[B_xg[i]], wr=[B_sq])
            for kc in range(KC):
                mm(st_ps[i][:, :], ones_bf[:], sq[:, kc, :], kc == 0, kc == KC - 1, rd=[B_sq, B_const], wb=B_st[i], first=(kc == 0), sig=(kc == KC - 1))
            act(rstd[i][:], st_ps[i][:, :], AF.Sqrt, rd=[B_st[i], B_const], wr=[B_rstd[i]], scale=1.0 / D, bias=eps_t[:])
            recip(rstd[i][:], rstd[i][:], rd=[B_rstd[i]], wr=[B_rstd[i]])
            for kc in range(KC):
                stt("dve", xg[i][:, kc, :], xg[i][:, kc, :], fg_s[:, kc:kc + 1], rstd[i][:], ALU.mult, ALU.mult,
                    rd=[B_rstd[i], B_const, B_sq], wr=[B_xg[i]] if kc == 0 else (), wr_more=() if kc == 0 else [B_xg[i]])
            dma("sp", yout[:, gi * 512:(gi + 1) * 512].rearrange("(c p) t -> p c t", p=128), xg[i][:], rd=[B_xg[i]])
        p.barrier()

    p.emit(nc)
    es.close()
    return nc


def prep_weights(L, norm1_g, norm2_g, w_ada, b_ada, w_in, out_norm_a, out_norm_b, w_out, na_rpb,
                 w_gate, w_up, w_down, final_g, t5_table):
    f = np.float32
    A = lambda a: np.ascontiguousarray(np.asarray(a, dtype=f))
    w = {}
    w["w_in_r"] = A(np.asarray(w_in)[:L].reshape(L, KC, 128, 12, 512).transpose(0, 3, 2, 1, 4))
    w["w_out_r"] = A(np.asarray(w_out)[:L].reshape(L, KC, 128, 4, 512).transpose(0, 3, 2, 1, 4))
    g_ = np.asarray(w_gate)[:L].reshape(L, KC, 128, FC, 128).transpose(0, 3, 2, 1, 4)
    u_ = np.asarray(w_up)[:L].reshape(L, KC, 128, FC, 128).transpose(0, 3, 2, 1, 4)
    w["w_gu_r"] = A(np.stack([g_, u_], axis=3))
    w["w_dn_r"] = A(np.asarray(w_down)[:L].reshape(L, 2, FH, 128, KC, 128).transpose(0, 1, 4, 3, 2, 5))
    w["w_ada_r"] = A(np.asarray(w_ada)[:L].reshape(L, KC, 128, 12, 8, 128).transpose(0, 3, 2, 4, 1, 5))
    w["b_ada_r"] = A(np.asarray(b_ada)[:L].reshape(L, 96, 128).transpose(0, 2, 1))
    w["n1g_r"] = A(np.asarray(norm1_g)[:L].reshape(L, KC, 128).transpose(0, 2, 1))
    w["n2g_r"] = A(np.asarray(norm2_g)[:L].reshape(L, KC, 128).transpose(0, 2, 1))
    ong = np.concatenate([np.asarray(out_norm_a)[:L], np.asarray(out_norm_b)[:L]], axis=1)
    w["ong_r"] = A(ong.reshape(L, KC, 128).transpose(0, 2, 1))
    w["fg_r"] = A(np.asarray(final_g).reshape(KC, 128).T)
    w["t5"] = A(t5_table)
    rp = np.asarray(na_rpb)[:L]
    w["rpbT"] = A(rp[:, :, ::-1, :].transpose(0, 3, 1, 2).reshape(L, 31, 120))
    w.update(static_tables())
    return w


def run_chunks(L, CH, chunks, weights, debug=False):
    W0 = CH + 2 * HT * L
    halo = HT * L * 128
    nc = build(L, CH, debug)
    in_maps = []
    for ci in range(8):
        m = dict(weights)
        xw = np.zeros((W0 * 128, D), np.float32)
        valid = np.zeros((W0 * 128,), np.float32)
        flags = np.zeros((128, 2), np.float32)
        cvec = np.zeros((D,), np.float32)
        if ci < len(chunks):
            xseq, c, a = chunks[ci]
            T = xseq.shape[0]
            lo, hi = a - halo, a + CH * 128 + halo
            s0, s1 = max(lo, 0), min(hi, T)
            xw[s0 - lo:s1 - lo] = xseq[s0:s1]
            valid[s0 - lo:s1 - lo] = 1.0
            flags[:, 0] = 1.0 if a == 0 else 0.0
            flags[:, 1] = 1.0 if a + CH * 128 == T else 0.0
            cvec = np.asarray(c, np.float32)
        m["xin"] = np.ascontiguousarray(xw.T)
        m["valid"] = np.ascontiguousarray(valid.reshape(W0, 128).T)
        m["flags"] = flags
        m["cvec"] = np.ascontiguousarray(cvec.reshape(KC, 128).T)
        in_maps.append(m)
    res = run_bass_kernel_spmd(nc, in_maps, core_ids=list(range(8)))
    if debug:
        return res.results
    return [np.ascontiguousarray(res.results[ci]["yout"].T) for ci in range(len(chunks))]


def kernel(x_prompt, x_sample, c_prompt, c_sample, t5_table, norm1_g, norm2_g, w_ada, b_ada,
           w_in, out_norm_a, out_norm_b, w_out, na_rpb, w_gate, w_up, w_down, final_g):
    L, CH = 4, 32
    x_prompt = np.asarray(x_prompt, np.float32)
    x_sample = np.asarray(x_sample, np.float32)
    c_prompt = np.asarray(c_prompt, np.float32)
    c_sample = np.asarray(c_sample, np.float32)
    weights = prep_weights(L, norm1_g, norm2_g, w_ada, b_ada, w_in, out_norm_a, out_norm_b, w_out, na_rpb,
                           w_gate, w_up, w_down, final_g, t5_table)
    chunks = [(x_prompt[0], c_prompt[0], i * CH * 128) for i in range(4)]
    chunks += [(x_sample[0], c_sample[0], 0), (x_sample[1], c_sample[1], 0)]
    outs = run_chunks(L, CH, chunks, weights)
    y_prompt = np.concatenate(outs[0:4], axis=0)[None].astype(np.float32)
    y_sample = np.stack([outs[4], outs[5]], axis=0).astype(np.float32)
    return (y_prompt, y_sample)
```

```python
import math
import contextlib
import numpy as np
import concourse.bass as bass
import concourse.mybir as mybir
from concourse.bass_utils import run_bass_kernel_spmd

F32 = mybir.dt.float32
BF16 = mybir.dt.bfloat16
AF = mybir.ActivationFunctionType
ALU = mybir.AluOpType

D = 2048
KC = 16
DFF = 5632
FC = 44
FH = 22
NH = 16
HD = 128
QK_SCALE = HD ** -0.5
EPS = 1e-6
HT = 8
BT = 8
GA_W = 2304
GB_W = 1920
N_BUCKETS = 32
MAX_DISTANCE = 1024

ENGINES = ["pe", "act", "dve", "pool", "sp"]
NDSEM = 8


class Buf:
    __slots__ = ("name", "w", "r")

    def __init__(self, name):
        self.name = name
        self.w = None
        self.r = []


class Prog:
    def __init__(self):
        self.ops = {e: [] for e in ENGINES}
        self.cnt = {}
        self.seen = {e: {} for e in ENGINES}
        self.dma_i = {"sp": 0, "pool": 0}
        self.pending = {e: False for e in ENGINES}
        for e in ENGINES:
            self.cnt["c_" + e] = 0
        for q in ("sp", "pool"):
            for i in range(NDSEM):
                self.cnt[f"d_{q}{i}"] = 0

    def _need(self, eng, ev, waits):
        if ev is None:
            return
        k, v = ev
        if self.seen[eng].get(k, 0) >= v:
            return
        self.seen[eng][k] = v
        waits[k] = max(waits.get(k, 0), v)

    def op(self, eng, fn, rd=(), wr=(), wr_more=(), dma=False, sig=True):
        waits = {}
        for b in rd:
            self._need(eng, b.w, waits)
        for b in wr:
            self._need(eng, b.w, waits)
            for ev in b.r:
                self._need(eng, ev, waits)
        if dma:
            i = self.dma_i[eng]
            self.dma_i[eng] = i + 1
            k = f"d_{eng}{i % NDSEM}"
            if self.cnt[k] > 0:
                self._need(eng, (k, self.cnt[k]), waits)
            self.cnt[k] += 16
            inc = (k, 16)
            ev = (k, self.cnt[k])
        else:
            k = "c_" + eng
            if sig:
                self.cnt[k] += 1
                inc = (k, 1)
                ev = (k, self.cnt[k])
                self.pending[eng] = False
            else:
                inc = None
                ev = (k, self.cnt[k] + 1)
                self.pending[eng] = True
        for b in rd:
            b.r.append(ev)
        for b in wr:
            b.w = ev
            b.r = []
        for b in wr_more:
            b.w = ev
        self.ops[eng].append((sorted(waits.items()), fn, inc))
        return ev

    def barrier(self):
        for e in ENGINES:
            assert not self.pending[e], e
        for e in ENGINES:
            waits = {}
            for k, v in self.cnt.items():
                if v > 0:
                    self._need(e, (k, v), waits)
            if waits:
                self.ops[e].append((sorted(waits.items()), None, None))

    def emit(self, nc):
        with contextlib.ExitStack() as st:
            sems = {k: st.enter_context(nc.semaphore(k)) for k in self.cnt}
            block = st.enter_context(nc.Block())

            def run(name):
                def body(eng):
                    for waits, fn, inc in self.ops[name]:
                        for k, v in waits:
                            eng.wait_ge(sems[k], v)
                        if fn is not None:
                            ins = fn(eng)
                            if inc is not None:
                                ins.then_inc(sems[inc[0]], inc[1])
                return body

            block.tensor(run("pe"))
            block.scalar(run("act"))
            block.vector(run("dve"))
            block.gpsimd(run("pool"))
            block.sync(run("sp"))


def t5_bucket(rel):
    nb = N_BUCKETS // 2
    exact = nb // 2
    n = np.abs(rel)
    sign = np.where(rel > 0, nb, 0)
    large = exact + (np.log(np.maximum(n, 1) / exact) / math.log(MAX_DISTANCE / exact) * (nb - exact)).astype(np.int64)
    large = np.minimum(large, nb - 1)
    return (sign + np.where(n < exact, n, large)).astype(np.int32)


def static_tables():
    u = np.arange(GA_W)
    o = 1151 - u
    ok = u <= 2302
    b = t5_bucket(o)
    ohA = np.zeros((32, GA_W), np.float32)
    ohA[b[ok], u[ok]] = 1.0
    ao = np.abs(o)
    mult = ((ao <= 64).astype(np.float32) + ((o % 4 == 0) & (ao <= 256)) + ((o % 16 == 0) & (ao <= 1024))).astype(np.float32)
    mult[~ok] = 0.0
    multA = np.tile(mult[None], (128, 1)).astype(np.float32)
    ub = np.arange(128)
    ohB = np.zeros((31, 128), np.float32)
    cidx = np.clip(63 - ub, -15, 15) + 15
    ohB[cidx[:127], ub[:127]] = 1.0
    qc = np.arange(64)
    cstart = np.clip(qc - 8, 0, 48)
    kcol = np.arange(64)
    cm = (kcol[:, None] >= cstart[None, :]) & (kcol[:, None] < cstart[None, :] + 16)
    cmaskT = np.concatenate([cm, cm], axis=0).astype(np.float32)
    ident = np.eye(128, dtype=np.float32)
    return dict(ohA=ohA, multA=multA, ohB=ohB, cmaskT=cmaskT, ident=ident)


def build(L, CH, debug=False):
    W0 = CH + 2 * HT * L
    WT = W0 * 128
    nc = bass.Bass("TRN2", target_bir_lowering=False)

    def din(name, shape, dtype=F32):
        return nc.dram_tensor(name, list(shape), dtype, kind="ExternalInput")

    def dscr(name, shape, dtype):
        return nc.dram_tensor(name, list(shape), dtype, kind="ExternalOutput" if debug else "Internal")

    xin = din("xin", [D, WT]).ap()
    valid_d = din("valid", [128, W0]).ap()
    flags_d = din("flags", [128, 2]).ap()
    cvec_d = din("cvec", [128, KC]).ap()
    w_in_d = din("w_in_r", [L, 12, 128, KC, 512]).ap()
    w_out_d = din("w_out_r", [L, 4, 128, KC, 512]).ap()
    w_gu_d = din("w_gu_r", [L, FC, 128, 2, KC, 128]).ap()
    w_dn_d = din("w_dn_r", [L, 2, KC, 128, FH, 128]).ap()
    w_ada_d = din("w_ada_r", [L, 12, 128, 8, KC, 128]).ap()
    b_ada_d = din("b_ada_r", [L, 128, 96]).ap()
    n1g_d = din("n1g_r", [L, 128, KC]).ap()
    n2g_d = din("n2g_r", [L, 128, KC]).ap()
    ong_d = din("ong_r", [L, 128, KC]).ap()
    fg_d = din("fg_r", [128, KC]).ap()
    t5_d = din("t5", [32, 8]).ap()
    rpbT_d = din("rpbT", [L, 31, 120]).ap()
    ohA_d = din("ohA", [32, GA_W]).ap()
    multA_d = din("multA", [128, GA_W]).ap()
    ohB_d = din("ohB", [31, 128]).ap()
    cmask_d = din("cmaskT", [128, 64]).ap()
    ident_d = din("ident", [128, 128]).ap()
    yout = nc.dram_tensor("yout", [D, CH * 128], F32, kind="ExternalOutput").ap()

    xs = dscr("xs", [D, WT], F32).ap()
    qTd = dscr("qTd", [NH, 128, WT], BF16).ap()
    kTd = dscr("kTd", [NH, 128, WT], BF16).ap()
    Vd = dscr("Vd", [NH, 128, W0, 130], BF16).ap()
    GdA_t = dscr("GdA", [8, 128, GA_W], BF16)
    GdA = GdA_t.ap()
    GdB_t = dscr("GdB", [8, 128, GB_W], BF16)
    GdB = GdB_t.ap()
    CCd = dscr("CCd", [L, 8, 128, 2 * 896], BF16).ap()

    p = Prog()
    es = contextlib.ExitStack()

    uid = [0]

    def sb(stack, name, shape, dtype):
        uid[0] += 1
        return stack.enter_context(nc.sbuf_tensor(f"s{uid[0]}_{name}", list(shape), dtype))

    def ps(stack, name, shape, dtype):
        uid[0] += 1
        return stack.enter_context(nc.psum_tensor(f"p{uid[0]}_{name}", list(shape), dtype))

    def dma(q, out, in_, rd=(), wr=()):
        p.op(q, lambda e: e.dma_start(out=out, in_=in_), rd=rd, wr=wr, dma=True)

    def mm(out, lhsT, rhs, start, stop, rd, wb, first, sig):
        p.op("pe", lambda e: e.matmul(out, lhsT=lhsT, rhs=rhs, start=start, stop=stop),
             rd=rd, wr=[wb] if first else (), wr_more=() if first else [wb], sig=sig)

    def act(out, in_, func, rd, wr, scale=None, bias=None, wr_more=()):
        kw = {}
        if scale is not None:
            kw["scale"] = scale
        if bias is not None:
            kw["bias"] = bias
        p.op("act", lambda e: e.activation(out=out, in_=in_, func=func, **kw), rd=rd, wr=wr, wr_more=wr_more)

    def actmul(out, in_, m, rd, wr, wr_more=()):
        p.op("act", lambda e: e.mul(out=out, in_=in_, mul=m), rd=rd, wr=wr, wr_more=wr_more)

    def actcopy(out, in_, rd, wr, wr_more=()):
        p.op("act", lambda e: e.copy(out=out, in_=in_), rd=rd, wr=wr, wr_more=wr_more)

    def tt(eng, out, in0, in1, op, rd, wr, wr_more=()):
        p.op(eng, lambda e: e.tensor_tensor(out=out, in0=in0, in1=in1, op=op), rd=rd, wr=wr, wr_more=wr_more)

    def ts(eng, out, in0, s1, op0, rd, wr, s2=None, op1=None, wr_more=()):
        if op1 is None:
            p.op(eng, lambda e: e.tensor_scalar(out=out, in0=in0, scalar1=s1, scalar2=None, op0=op0),
                 rd=rd, wr=wr, wr_more=wr_more)
        else:
            p.op(eng, lambda e: e.tensor_scalar(out=out, in0=in0, scalar1=s1, scalar2=s2, op0=op0, op1=op1),
                 rd=rd, wr=wr, wr_more=wr_more)

    def stt(eng, out, in0, scalar, in1, op0, op1, rd, wr, wr_more=()):
        p.op(eng, lambda e: e.scalar_tensor_tensor(out=out, in0=in0, scalar=scalar, in1=in1, op0=op0, op1=op1),
             rd=rd, wr=wr, wr_more=wr_more)

    def recip(out, in_, rd, wr):
        p.op("dve", lambda e: e.reciprocal(out=out, in_=in_), rd=rd, wr=wr)

    def copy(eng, out, in_, rd, wr, wr_more=()):
        p.op(eng, lambda e: e.tensor_copy(out=out, in_=in_), rd=rd, wr=wr, wr_more=wr_more)

    def memset(eng, ap, val, wr):
        p.op(eng, lambda e: e.memset(ap, val), wr=wr)

    def transpose(out, in_, ident, rd, wb):
        p.op("pe", lambda e: e.transpose(out, in_, ident), rd=rd, wr=[wb])

    ones_bf = sb(es, "ones_bf", [128, 128], BF16)
    ident_bf = sb(es, "ident_bf", [128, 128], BF16)
    eps_t = sb(es, "eps_t", [128, 1], F32)
    tiny_t = sb(es, "tiny_t", [128, 1], F32)
    valid_s = sb(es, "valid_s", [128, W0], F32)
    flags_s = sb(es, "flags_s", [128, 2], F32)
    modc = sb(es, "modc", [128, L, 96], F32)
    gm = sb(es, "gm", [128, L, 32], F32)
    ong_s = sb(es, "ong_s", [128, L, KC], F32)
    fg_s = sb(es, "fg_s", [128, KC], F32)
    onescol = sb(es, "onescol", [128, 4, 1], F32)
    B_const = Buf("const")

    mm_ps = [ps(es, f"mm{i}", [128, 512], F32) for i in range(2)]
    mu_ps = [ps(es, f"mu{i}", [128, 512], F32) for i in range(2)]
    st_ps = [ps(es, f"st{i}", [128, 512], F32) for i in range(2)]
    pv_ps = ps(es, "pv", [128, 2, 256], F32)
    tp_ps = ps(es, "tp", [128, 2, 512], BF16)
    B_mm = [Buf(f"mm{i}") for i in range(2)]
    B_mu = [Buf(f"mu{i}") for i in range(2)]
    B_st = [Buf(f"st{i}") for i in range(2)]
    B_pv = [Buf(f"pv{i}") for i in range(2)]
    B_tp = [Buf(f"tp{i}") for i in range(2)]

    actT = sb(es, "actT", [128, KC, BT * 128], BF16)
    B_act = Buf("actT")
    slab = [sb(es, f"slab{i}", [128, KC, 512], BF16) for i in range(2)]
    B_slab = [Buf(f"slab{i}") for i in range(2)]
    slab_i = [0]
    wgu = [sb(es, f"wgu{i}", [128, 2, KC, 128], BF16) for i in range(2)]
    B_wgu = [Buf("wgu0"), Buf("wgu1")]
    pre = {}

    def next_slab():
        i = slab_i[0] % 2
        slab_i[0] += 1
        return slab[i], B_slab[i]

    rot = {}

    def rotate(key, n):
        i = rot.get(key, 0)
        rot[key] = i + 1
        return i % n

    memset("pool", ones_bf[:], 1.0, [B_const])
    memset("pool", eps_t[:], EPS, [B_const])
    memset("pool", tiny_t[:], 1e-30, [B_const])
    memset("pool", onescol[:], 1.0, [B_const])
    dma("pool", ident_bf[:], ident_d[:, :], wr=[B_const])
    dma("sp", valid_s[:], valid_d[:, :], wr=[B_const])
    dma("sp", flags_s[:], flags_d[:, :], wr=[B_const])
    dma("sp", fg_s[:], fg_d[:, :], wr=[B_const])
    for l in range(L):
        dma("sp", ong_s[:, l, :], ong_d[l], wr=[B_const])
    p.barrier()

    with contextlib.ExitStack() as s0:
        cv = sb(s0, "cv", [128, KC], F32)
        csil = sb(s0, "csil", [128, KC], BF16)
        bada = sb(s0, "bada", [128, L, 96], F32)
        n1g = sb(s0, "n1g", [128, L, KC], F32)
        n2g = sb(s0, "n2g", [128, L, KC], F32)
        wa = [sb(s0, f"wa{i}", [128, 8, KC, 128], BF16) for i in range(2)]
        B_wa = [Buf("wa0"), Buf("wa1")]
        B_c0 = Buf("c0")
        dma("sp", cv[:], cvec_d[:, :], wr=[B_c0])
        for l in range(L):
            dma("sp", bada[:, l, :], b_ada_d[l], wr=[B_c0])
            dma("sp", n1g[:, l, :], n1g_d[l], wr=[B_c0])
            dma("sp", n2g[:, l, :], n2g_d[l], wr=[B_c0])
        p.barrier()
        act(csil[:], cv[:], AF.Silu, rd=[B_c0], wr=[B_c0])
        for l in range(L):
            for jg in range(12):
                i = (l * 12 + jg) % 2
                dma("pool", wa[i][:], w_ada_d[l, jg], wr=[B_wa[i]])
                for j in range(8):
                    col = jg * 8 + j
                    for kc in range(KC):
                        mm(st_ps[0][:, col:col + 1], wa[i][:, j, kc, :], csil[:, kc:kc + 1],
                           kc == 0, kc == KC - 1, rd=[B_wa[i], B_c0], wb=B_st[0],
                           first=(jg == 0 and j == 0 and kc == 0), sig=(kc == KC - 1))
            tt("dve", modc[:, l, :], st_ps[0][:, 0:96], bada[:, l, :], ALU.add, rd=[B_st[0], B_c0], wr=[B_const])
            stt("dve", gm[:, l, 0:16], modc[:, l, 16:32], 1.0, n1g[:, l, :], ALU.add, ALU.mult, rd=[B_const, B_c0], wr=[B_const])
            stt("dve", gm[:, l, 16:32], modc[:, l, 64:80], 1.0, n2g[:, l, :], ALU.add, ALU.mult, rd=[B_const, B_c0], wr=[B_const])
        p.barrier()

    with contextlib.ExitStack() as s0:
        t5s = sb(s0, "t5s", [32, 8], F32)
        ohA = sb(s0, "ohA", [32, GA_W], BF16)
        multA = sb(s0, "multA", [128, GA_W], F32)
        ones32 = sb(s0, "ones32", [32, 128], F32)
        lh = [sb(s0, f"lh{i}", [32, 128], BF16) for i in range(2)]
        gA = [sb(s0, f"gA{i}", [128, GA_W], BF16) for i in range(2)]
        etmp = [sb(s0, f"etmp{i}", [128, 512], F32) for i in range(2)]
        B_m = Buf("m")
        B_lh = [Buf("lh0"), Buf("lh1")]
        B_gA = [Buf("gA0"), Buf("gA1")]
        B_et = [Buf("et0"), Buf("et1")]
        dma("sp", t5s[:], t5_d[:, :], wr=[B_m])
        dma("pool", ohA[:], ohA_d[:, :], wr=[B_m])
        dma("sp", multA[:], multA_d[:, :], wr=[B_m])
        memset("pool", ones32[:], 1.0, [B_m])
        p.barrier()
        for h in range(8):
            i = h % 2
            ts("dve", lh[i][:], ones32[:], t5s[:, h:h + 1], ALU.mult, rd=[B_m], wr=[B_lh[i]])
            for c in range(5):
                n = min(512, GA_W - c * 512)
                j = rotate("pA", 2)
                mm(mm_ps[j][:, 0:n], lh[i][:], ohA[:, c * 512:c * 512 + n], True, True, rd=[B_lh[i], B_m], wb=B_mm[j], first=True, sig=True)
                act(etmp[j][:, 0:n], mm_ps[j][:, 0:n], AF.Exp, rd=[B_mm[j]], wr=[B_et[j]])
                tt("dve", gA[i][:, c * 512:c * 512 + n], etmp[j][:, 0:n], multA[:, c * 512:c * 512 + n], ALU.mult,
                   rd=[B_et[j], B_m], wr=[B_gA[i]] if c == 0 else (), wr_more=() if c == 0 else [B_gA[i]])
            dma("sp", GdA[h], gA[i][:], rd=[B_gA[i]])
        p.barrier()

    with contextlib.ExitStack() as s0:
        rpbs = sb(s0, "rpbs", [31, L, 120], F32)
        ohB = sb(s0, "ohB", [31, 128], BF16)
        ones31 = sb(s0, "ones31", [31, 128], F32)
        cmask = sb(s0, "cmask", [128, 64], F32)
        lhb = [sb(s0, f"lhb{i}", [31, 128], BF16) for i in range(2)]
        gB = [sb(s0, f"gB{i}", [128, GB_W], BF16) for i in range(2)]
        ccf = [sb(s0, f"ccf{i}", [128, 14, 64], BF16) for i in range(2)]
        ccm = [sb(s0, f"ccm{i}", [128, 14, 64], BF16) for i in range(2)]
        cc2 = [sb(s0, f"cc2{i}", [128, 2, 14, 64], BF16) for i in range(2)]
        B_m = Buf("mB")
        B_lhb = [Buf("lhb0"), Buf("lhb1")]
        B_gB = [Buf("gB0"), Buf("gB1")]
        B_ccf = [[Buf("ccf00"), Buf("ccf01")], [Buf("ccf10"), Buf("ccf11")]]
        B_ccm = [Buf("ccm0"), Buf("ccm1")]
        B_cc2 = [Buf("cc20"), Buf("cc21")]
        B_GdB = [Buf(f"GdB{h}") for h in range(8)]
        for l in range(L):
            dma("sp", rpbs[:, l, :], rpbT_d[l], wr=[B_m])
        dma("pool", ohB[:], ohB_d[:, :], wr=[B_m])
        dma("sp", cmask[:], cmask_d[:, :], wr=[B_m])
        memset("pool", ones31[:], 1.0, [B_m])
        p.barrier()
        for l in range(L):
            for h in range(8):
                i = (l * 8 + h) % 2
                for sp_ in range(15):
                    jb = rotate("lhb", 2)
                    ts("dve", lhb[jb][:], ones31[:], rpbs[:, l, h * 15 + sp_:h * 15 + sp_ + 1], ALU.mult, rd=[B_m], wr=[B_lhb[jb]])
                    if sp_ % 4 == 0:
                        j = rotate("pA", 2)
                    mm(mm_ps[j][:, (sp_ % 4) * 128:(sp_ % 4 + 1) * 128], lhb[jb][:], ohB[:], True, True,
                       rd=[B_lhb[jb], B_m], wb=B_mm[j], first=(sp_ % 4 == 0), sig=True)
                    if sp_ % 4 == 3 or sp_ == 14:
                        s0_ = (sp_ // 4) * 4
                        n = (sp_ - s0_ + 1) * 128
                        act(gB[i][:, s0_ * 128:s0_ * 128 + n], mm_ps[j][:, 0:n], AF.Exp, rd=[B_mm[j]],
                            wr=[B_gB[i]] if s0_ == 0 else (), wr_more=() if s0_ == 0 else [B_gB[i]])
                dma("sp", GdB[h], gB[i][:], rd=[B_gB[i]], wr=[B_GdB[h]])
                for kr in range(2):
                    src = bass.AP(GdB_t, h * 128 * GB_W + (1 - kr) * 128 + 63, [[GB_W - 1, 64], [128, 14], [1, 64]])
                    dma("sp", ccf[i][kr * 64:(kr + 1) * 64, :, :], src, rd=[B_GdB[h]], wr=[B_ccf[i][kr]])
                for s_ in range(14):
                    tt("dve", ccm[i][:, s_, :], ccf[i][:, s_, :], cmask[:], ALU.mult, rd=[B_ccf[i][0], B_ccf[i][1], B_m],
                       wr=[B_ccm[i]] if s_ == 0 else (), wr_more=() if s_ == 0 else [B_ccm[i]])
                memset("pool", cc2[i][:], 0.0, [B_cc2[i]])
                cps = [(0, 0, 3, 11), (64, 0, 4, 12), (0, 1, 0, 3), (0, 1, 11, 14), (64, 1, 0, 4), (64, 1, 12, 14)]
                for (p0, w, a, b_) in cps:
                    copy("dve", cc2[i][p0:p0 + 64, w, a:b_, :], ccm[i][p0:p0 + 64, a:b_, :], rd=[B_ccm[i], B_cc2[i]], wr=(), wr_more=[B_cc2[i]])
                dma("sp", CCd[l, h], cc2[i][:], rd=[B_cc2[i]])
        p.barrier()

    def norm_block(l, which, src_ap, tok0, xg, B_xg, sq, B_sq, tmpn, B_tmpn, rstd, B_rstd, g, load=True):
        if load:
            dma("pool", xg[:], src_ap[:, tok0:tok0 + 512].rearrange("(c p) t -> p c t", p=128), wr=[B_xg])
        tt("pool", sq[:], xg[:], xg[:], ALU.mult, rd=[B_xg], wr=[B_sq])
        for kc in range(KC):
            mm(st_ps[0][:, :], ones_bf[:], sq[:, kc, :], kc == 0, kc == KC - 1, rd=[B_sq, B_const], wb=B_st[0], first=(kc == 0), sig=(kc == KC - 1))
        act(rstd[:], st_ps[0][:, :], AF.Sqrt, rd=[B_st[0], B_const], wr=[B_rstd], scale=1.0 / D, bias=eps_t[:])
        recip(rstd[:], rstd[:], rd=[B_rstd], wr=[B_rstd])
        goff = 0 if which == 1 else 16
        soff = 0 if which == 1 else 48
        for kc in range(KC):
            j = rotate("tmpn", 2)
            tt("dve", tmpn[j][:], xg[:, kc, :], rstd[:], ALU.mult, rd=[B_xg, B_rstd], wr=[B_tmpn[j]])
            act(actT[:, kc, g * 512:(g + 1) * 512], tmpn[j][:], AF.Identity, rd=[B_tmpn[j], B_const],
                wr=[B_act] if (kc == 0 and g == 0) else (), wr_more=() if (kc == 0 and g == 0) else [B_act],
                scale=gm[:, l, goff + kc:goff + kc + 1], bias=modc[:, l, soff + kc:soff + kc + 1])

    for l in range(L):
        kv0, kv1 = HT * l, W0 - HT * l
        o0, o1 = HT * (l + 1), W0 - HT * (l + 1)
        xsrc = xin if l == 0 else xs

        with contextlib.ExitStack() as s1:
            xg = [sb(s1, f"xg{i}", [128, KC, 512], F32) for i in range(2)]
            B_xg = [Buf("xg0"), Buf("xg1")]
            sq = sb(s1, "sq", [128, KC, 512], BF16)
            B_sq = Buf("sq")
            tmpn = [sb(s1, f"tmpn{i}", [128, 512], F32) for i in range(2)]
            B_tmpn = [Buf("tmpn0"), Buf("tmpn1")]
            rstd = sb(s1, "rstd", [128, 512], F32)
            B_rstd = Buf("rstd")
            kst = [sb(s1, f"kst{i}", [128, 512], BF16) for i in range(3)]
            B_kst = [Buf(f"kst{i}") for i in range(3)]
            vst = [sb(s1, f"vst{i}", [128, 4, BT, 130], BF16) for i in range(2)]
            B_vst = [Buf("vst0"), Buf("vst1")]
            B_vone = [Buf("vone0"), Buf("vone1")]
            for b0 in range(kv0, kv1, BT):
                tok0 = b0 * 128
                need_q = (o0 <= b0 < o1)
                for g in range(2):
                    norm_block(l, 1, xsrc, tok0 + g * 512, xg[g], B_xg[g], sq, B_sq, tmpn, B_tmpn, rstd, B_rstd, g)
                for s in range(12):
                    kind = ("q", "k", "v")[(s % 6) // 2]
                    if kind == "q" and not need_q:
                        continue
                    sl, B_sl = next_slab()
                    dma("pool", sl[:], w_in_d[l, s], wr=[B_sl])
                    hbase = (s // 6) * 8 + (s % 2) * 4
                    if kind in ("q", "k"):
                        dst = qTd if kind == "q" else kTd
                        for cc in range(4):
                            for g in range(2):
                                j = rotate("mm", 2)
                                for kc in range(KC):
                                    mm(mm_ps[j][:, :], sl[:, kc, cc * 128:(cc + 1) * 128], actT[:, kc, g * 512:(g + 1) * 512],
                                       kc == 0, kc == KC - 1, rd=[B_sl, B_act], wb=B_mm[j], first=(kc == 0), sig=(kc == KC - 1))
                                ks = rotate("kst", 3)
                                if kind == "q":
                                    actmul(kst[ks][:], mm_ps[j][:, :], QK_SCALE, rd=[B_mm[j]], wr=[B_kst[ks]])
                                else:
                                    copy("dve", kst[ks][:], mm_ps[j][:, :], rd=[B_mm[j]], wr=[B_kst[ks]])
                                dma("sp", dst[hbase + cc, :, tok0 + g * 512:tok0 + (g + 1) * 512], kst[ks][:], rd=[B_kst[ks]])
                    else:
                        vi = rotate("vst", 2)
                        for t in range(BT):
                            j = rotate("mm", 2)
                            for kc in range(KC):
                                mm(mm_ps[j][:, :], actT[:, kc, t * 128:(t + 1) * 128], sl[:, kc, :],
                                   kc == 0, kc == KC - 1, rd=[B_sl, B_act], wb=B_mm[j], first=(kc == 0), sig=(kc == KC - 1))
                            vcol = valid_s[:, b0 + t:b0 + t + 1]
                            ts("dve", vst[vi][:, :, t, 0:128], mm_ps[j][:, :].rearrange("p (h d) -> p h d", h=4), vcol, ALU.mult,
                               rd=[B_mm[j], B_const], wr=[B_vst[vi]] if t == 0 else (), wr_more=() if t == 0 else [B_vst[vi]])
                            ts("pool", vst[vi][:, :, t, 128:129], onescol[:], vcol, ALU.mult, rd=[B_const],
                               wr=[B_vone[vi]] if t == 0 else (), wr_more=() if t == 0 else [B_vone[vi]])
                        for hh in range(4):
                            dma("sp", Vd[hbase + hh, :, b0:b0 + BT, :], vst[vi][:, hh, :, :], rd=[B_vst[vi], B_vone[vi]])
            p.barrier()

        for q0 in range(o0, o1, BT):
            tok0 = q0 * 128
            spec = {}
            c0 = HT * L
            spec[c0] = ("s0", 0)
            spec[c0 + 1] = ("s1", 0)
            spec[c0 + CH - 2] = ("e1", 1)
            spec[c0 + CH - 1] = ("e0", 1)
            with contextlib.ExitStack() as s3:
                kwin = [sb(s3, f"kwin{i}", [128, 24 * 128], BF16) for i in range(2)]
                vwin = [sb(s3, f"vwin{i}", [128, 24, 130], BF16) for i in range(2)]
                qwin = [sb(s3, f"qwin{i}", [128, BT * 128], BF16) for i in range(2)]
                em = [sb(s3, f"em{i}", [128, 2176], BF16) for i in range(2)]
                mb = [sb(s3, f"mb{i}", [128, 896], BF16) for i in range(2)]
                Pb = [sb(s3, f"Pb{i}", [128, 512], BF16) for i in range(4)]
                Pm = [sb(s3, f"Pm{i}", [128, 512], BF16) for i in range(5)]
                yst = [sb(s3, f"yst{i}", [128, 128], BF16) for i in range(2)]
                rden = [sb(s3, f"rden{i}", [128, 1], F32) for i in range(2)]
                B_kw = [Buf("kw0"), Buf("kw1")]
                B_vw = [Buf("vw0"), Buf("vw1")]
                B_qw = [Buf("qw0"), Buf("qw1")]
                B_em = [Buf("em0"), Buf("em1")]
                B_mb = [Buf("mb0"), Buf("mb1")]
                B_Pb = [Buf(f"Pb{i}") for i in range(4)]
                B_Pm = [Buf(f"Pm{i}") for i in range(5)]
                B_yst = [Buf("yst0"), Buf("yst1")]
                B_rd = [Buf("rd0"), Buf("rd1")]
                first_act = [True]
                for s_ in range(2):
                    sl_, B_sl_ = next_slab()
                    dma("pool", sl_[:], w_out_d[l, s_], wr=[B_sl_])
                    pre[("wo", s_)] = (sl_, B_sl_)
                pend = []
                LA = 3
                S_ps = [mm_ps[0], mm_ps[1], mu_ps[0], mu_ps[1]]
                B_S = [B_mm[0], B_mm[1], B_mu[0], B_mu[1]]

                def emit_pv(tk):
                    (bi_, hd_, t_, hk_, gp_, pm_, vb_, done0, ntot_) = tk
                    done_ = done0
                    for i_, dl in enumerate(gp_):
                        kw_ = t_ + hk_ + dl
                        mm(pv_ps[:, vb_, 0:129], Pm[pm_][:, i_ * 128:(i_ + 1) * 128], vwin[bi_][:, kw_, 0:129],
                           done_ == 0, done_ == ntot_ - 1, rd=[B_Pm[pm_], B_vw[bi_]], wb=B_pv[vb_],
                           first=(done_ == 0), sig=(done_ == ntot_ - 1))
                        done_ += 1
                    if done_ == ntot_:
                        ri = rotate("rden", 2)
                        ts("dve", rden[ri][:], pv_ps[:, vb_, 128:129], tiny_t[:, 0:1], ALU.max, rd=[B_pv[vb_], B_const], wr=[B_rd[ri]])
                        recip(rden[ri][:], rden[ri][:], rd=[B_rd[ri]], wr=[B_rd[ri]])
                        yi = rotate("yst", 2)
                        ts("dve", yst[yi][:], pv_ps[:, vb_, 0:128], rden[ri][:, 0:1], ALU.mult, rd=[B_pv[vb_], B_rd[ri]], wr=[B_yst[yi]])
                        ti = rotate("tp", 2)
                        transpose(tp_ps[:, ti, 0:128], yst[yi][:], ident_bf[:], rd=[B_yst[yi], B_const], wb=B_tp[ti])
                        actcopy(actT[:, hd_, t_ * 128:(t_ + 1) * 128], tp_ps[:, ti, 0:128], rd=[B_tp[ti]],
                                wr=[B_act] if first_act[0] else (), wr_more=() if first_act[0] else [B_act])
                        first_act[0] = False

                for hd in range(NH):
                    isA = hd < 8
                    hk = 8 if isA else 3
                    nw = BT + 2 * hk
                    bi = hd % 2
                    dma("sp", kwin[bi][:, 0:nw * 128], kTd[hd, :, (q0 - hk) * 128:(q0 + BT + hk) * 128], wr=[B_kw[bi]])
                    dma("sp", vwin[bi][:, 0:nw, :], Vd[hd, :, q0 - hk:q0 + BT + hk, :], wr=[B_vw[bi]])
                    dma("sp", qwin[bi][:], qTd[hd, :, tok0:tok0 + BT * 128], wr=[B_qw[bi]])
                    if isA:
                        src = bass.AP(GdA_t, hd * 128 * GA_W + 127, [[GA_W - 1, 128], [1, 2176]])
                        dma("sp", em[bi][:], src, wr=[B_em[bi]])
                    else:
                        dma("sp", em[bi][:, 0:1792], CCd[l, hd - 8], wr=[B_em[bi]])
                    mb_ready = {}
                    for t in range(BT):
                        qt = q0 + t
                        if isA:
                            groups = [[8, 7, 6, 5], [4, 3, 2, 1], [0, -1, -2, -3], [-4, -5, -6, -7], [-8]]
                            msrc = lambda dmax, n: (em[bi][:, 128 * (8 - dmax):128 * (8 - dmax) + 128 * n], B_em[bi])
                        else:
                            kind = spec.get(qt)
                            if kind is None:
                                groups = [[2, 1, 0, -1], [-2]]
                                msrc = lambda dmax, n: (em[bi][:, (6 - 2 * dmax) * 64:(6 - 2 * dmax) * 64 + 128 * n], B_em[bi])
                            else:
                                nm, fl = kind
                                if fl not in mb_ready:
                                    mi = rotate("mb", 2)
                                    stt("dve", mb[mi][:], em[bi][:, 896:1792], flags_s[:, fl:fl + 1], em[bi][:, 0:896],
                                        ALU.mult, ALU.add, rd=[B_em[bi], B_const], wr=[B_mb[mi]])
                                    mb_ready[fl] = mi
                                mi = mb_ready[fl]
                                groups = {"s0": [[3, 2, 1, 0], [-1, -2]], "s1": [[2, 1, 0, -1], [-2]],
                                          "e1": [[2, 1, 0, -1], [-2]], "e0": [[2, 1, 0, -1], [-2, -3]]}[nm]
                                msrc = (lambda mi_: (lambda dmax, n: (mb[mi_][:, (6 - 2 * dmax) * 64:(6 - 2 * dmax) * 64 + 128 * n], B_mb[mi_])))(mi)
                        vb = rotate("pv", 2)
                        ntot = sum(len(gp) for gp in groups)
                        done = 0
                        for gp in groups:
                            n = len(gp)
                            sbk = rotate("Sbank", 4)
                            for i_, dl in enumerate(gp):
                                kw_ = t + hk + dl
                                mm(S_ps[sbk][:, i_ * 128:(i_ + 1) * 128], kwin[bi][:, kw_ * 128:(kw_ + 1) * 128],
                                   qwin[bi][:, t * 128:(t + 1) * 128], True, True, rd=[B_kw[bi], B_qw[bi]], wb=B_S[sbk],
                                   first=(i_ == 0), sig=(i_ == n - 1))
                            pb = rotate("Pb", 4)
                            act(Pb[pb][:, 0:n * 128], S_ps[sbk][:, 0:n * 128], AF.Exp, rd=[B_S[sbk]], wr=[B_Pb[pb]])
                            pm = rotate("Pm", 5)
                            map_, B_map = msrc(gp[0], n)
                            tt("dve", Pm[pm][:, 0:n * 128], Pb[pb][:, 0:n * 128], map_, ALU.mult, rd=[B_Pb[pb], B_map], wr=[B_Pm[pm]])
                            pend.append((bi, hd, t, hk, gp, pm, vb, done, ntot))
                            done += n
                            if len(pend) > LA:
                                emit_pv(pend.pop(0))
                while pend:
                    emit_pv(pend.pop(0))
                p.barrier()

            with contextlib.ExitStack() as s4:
                for ffc_ in range(2):
                    wi_ = rotate("wgu", 2)
                    dma("pool", wgu[wi_][:], w_gu_d[l, ffc_], wr=[B_wgu[wi_]])
                    pre[("gu", ffc_)] = wi_
                xnew = sb(s4, "xnew", [128, 2, KC, 512], F32)
                B_xn = [Buf("xn0"), Buf("xn1")]
                for g_ in range(2):
                    dma("sp", xnew[:, g_, :, :], xsrc[:, tok0 + g_ * 512:tok0 + (g_ + 1) * 512].rearrange("(c p) t -> p c t", p=128), wr=[B_xn[g_]])
                sqy = sb(s4, "sqy", [128, KC, 512], BF16)
                B_sqy = Buf("sqy")
                rs = [sb(s4, f"rs{i}", [128, 512], F32) for i in range(2)]
                B_rs = [Buf("rs0"), Buf("rs1")]
                xp = [sb(s4, f"xp{i}", [128, 512], F32) for i in range(3)]
                B_xp = [Buf(f"xp{i}") for i in range(3)]
                sqp = [sb(s4, f"sqp{i}", [128, 512], BF16) for i in range(2)]
                B_sqp = [Buf("sqp0"), Buf("sqp1")]
                tmpn = [sb(s4, f"tmq{i}", [128, 512], F32) for i in range(2)]
                B_tmpn = [Buf("tmq0"), Buf("tmq1")]
                rstd2 = [sb(s4, f"rstd2{i}", [128, 512], F32) for i in range(2)]
                B_rstd2 = [Buf("r20"), Buf("r21")]
                for g in range(2):
                    tt("pool", sqy[:], actT[:, :, g * 512:(g + 1) * 512], actT[:, :, g * 512:(g + 1) * 512], ALU.mult, rd=[B_act], wr=[B_sqy])
                    for half in range(2):
                        for kk in range(8):
                            kc = half * 8 + kk
                            mm(st_ps[half][:, :], ones_bf[:], sqy[:, kc, :], kk == 0, kk == 7, rd=[B_sqy, B_const], wb=B_st[half], first=(kk == 0), sig=(kk == 7))
                        act(rs[half][:], st_ps[half][:, :], AF.Sqrt, rd=[B_st[half], B_const], wr=[B_rs[half]], scale=1.0 / 1024, bias=eps_t[:])
                        recip(rs[half][:], rs[half][:], rd=[B_rs[half]], wr=[B_rs[half]])
                    for kc in range(KC):
                        half = kc // 8
                        stt("dve", actT[:, kc, g * 512:(g + 1) * 512], actT[:, kc, g * 512:(g + 1) * 512], ong_s[:, l, kc:kc + 1], rs[half][:],
                            ALU.mult, ALU.mult, rd=[B_rs[half], B_const, B_act], wr=[B_act] if kc == 0 else (), wr_more=() if kc == 0 else [B_act])
                for s in range(4):
                    if ("wo", s) in pre:
                        sl, B_sl = pre.pop(("wo", s))
                    else:
                        sl, B_sl = next_slab()
                        dma("pool", sl[:], w_out_d[l, s], wr=[B_sl])
                    for cc in range(4):
                        ch = s * 4 + cc
                        for g in range(2):
                            j = rotate("mm", 2)
                            for kc in range(KC):
                                mm(mm_ps[j][:, :], sl[:, kc, cc * 128:(cc + 1) * 128], actT[:, kc, g * 512:(g + 1) * 512],
                                   kc == 0, kc == KC - 1, rd=[B_sl, B_act], wb=B_mm[j], first=(kc == 0), sig=(kc == KC - 1))
                            stt("dve", xnew[:, g, ch, :], mm_ps[j][:, :], modc[:, l, 32 + ch:33 + ch], xnew[:, g, ch, :], ALU.mult, ALU.add,
                                rd=[B_mm[j], B_xn[g], B_const], wr=(), wr_more=[B_xn[g]])
                            si = rotate("sqp", 2)
                            tt("pool", sqp[si][:], xnew[:, g, ch, :], xnew[:, g, ch, :], ALU.mult, rd=[B_xn[g]], wr=[B_sqp[si]])
                            mm(st_ps[g][:, :], ones_bf[:], sqp[si][:], ch == 0, ch == KC - 1, rd=[B_sqp[si], B_const], wb=B_st[g], first=(ch == 0), sig=True)
                for g in range(2):
                    dma("sp", xs[:, tok0 + g * 512:tok0 + (g + 1) * 512].rearrange("(c p) t -> p c t", p=128), xnew[:, g, :, :], rd=[B_xn[g]])
                for g in range(2):
                    act(rstd2[g][:], st_ps[g][:, :], AF.Sqrt, rd=[B_st[g], B_const], wr=[B_rstd2[g]], scale=1.0 / D, bias=eps_t[:])
                    recip(rstd2[g][:], rstd2[g][:], rd=[B_rstd2[g]], wr=[B_rstd2[g]])
                    for kc in range(KC):
                        j = rotate("tmq", 2)
                        tt("dve", tmpn[j][:], xnew[:, g, kc, :], rstd2[g][:], ALU.mult, rd=[B_xn[g], B_rstd2[g]], wr=[B_tmpn[j]])
                        act(actT[:, kc, g * 512:(g + 1) * 512], tmpn[j][:], AF.Identity, rd=[B_tmpn[j], B_const],
                            wr=[B_act] if (kc == 0 and g == 0) else (), wr_more=() if (kc == 0 and g == 0) else [B_act],
                            scale=gm[:, l, 16 + kc:17 + kc], bias=modc[:, l, 48 + kc:49 + kc])
                p.barrier()

            with contextlib.ExitStack() as s6:
                aT = sb(s6, "aT", [128, FH, BT * 128], BF16)
                B_aT = Buf("aT")
                wd = [sb(s6, f"wd{i}", [128, FH, 128], BF16) for i in range(2)]
                B_wd = [Buf("wd0"), Buf("wd1")]
                sg = [sb(s6, f"sg{i}", [128, 512], F32) for i in range(2)]
                B_sg = [Buf("sg0"), Buf("sg1")]
                xp = [sb(s6, f"xq{i}", [128, 512], F32) for i in range(3)]
                B_xp = [Buf(f"xq{i}") for i in range(3)]
                xo = [sb(s6, f"xo{i}", [128, 512], F32) for i in range(3)]
                B_xo = [Buf(f"xo{i}") for i in range(3)]
                B_piece = [[Buf(f"pc{c}_{g}") for g in range(2)] for c in range(KC)]
                for fh in range(2):
                    for fi in range(FH):
                        ffc = fh * FH + fi
                        if ("gu", ffc) in pre:
                            wi = pre.pop(("gu", ffc))
                        else:
                            wi = rotate("wgu", 2)
                            dma("pool", wgu[wi][:], w_gu_d[l, ffc], wr=[B_wgu[wi]])
                        for g in range(2):
                            j = rotate("mm", 2)
                            for kc in range(KC):
                                mm(mm_ps[j][:, :], wgu[wi][:, 0, kc, :], actT[:, kc, g * 512:(g + 1) * 512], kc == 0, kc == KC - 1,
                                   rd=[B_wgu[wi], B_act], wb=B_mm[j], first=(kc == 0), sig=(kc == KC - 1))
                            ju = rotate("mu", 2)
                            for kc in range(KC):
                                mm(mu_ps[ju][:, :], wgu[wi][:, 1, kc, :], actT[:, kc, g * 512:(g + 1) * 512], kc == 0, kc == KC - 1,
                                   rd=[B_wgu[wi], B_act], wb=B_mu[ju], first=(kc == 0), sig=(kc == KC - 1))
                            gi = rotate("sg", 2)
                            act(sg[gi][:], mm_ps[j][:, :], AF.Silu, rd=[B_mm[j]], wr=[B_sg[gi]])
                            tt("dve", aT[:, fi, g * 512:(g + 1) * 512], sg[gi][:], mu_ps[ju][:, :], ALU.mult, rd=[B_sg[gi], B_mu[ju]],
                               wr=[B_aT] if (fi == 0 and g == 0) else (), wr_more=() if (fi == 0 and g == 0) else [B_aT])
                    for ch in range(KC):
                        wi = rotate("wd", 2)
                        dma("pool", wd[wi][:], w_dn_d[l, fh, ch], wr=[B_wd[wi]])
                        for g in range(2):
                            j = rotate("mm", 2)
                            for fi in range(FH):
                                mm(mm_ps[j][:, :], wd[wi][:, fi, :], aT[:, fi, g * 512:(g + 1) * 512], fi == 0, fi == FH - 1,
                                   rd=[B_wd[wi], B_aT], wb=B_mm[j], first=(fi == 0), sig=(fi == FH - 1))
                            xi = rotate("xq", 3)
                            xsl = xs[ch * 128:(ch + 1) * 128, tok0 + g * 512:tok0 + (g + 1) * 512]
                            dma("pool", xp[xi][:], xsl, rd=[B_piece[ch][g]], wr=[B_xp[xi]])
                            oi = rotate("xo", 3)
                            stt("dve", xo[oi][:], mm_ps[j][:, :], modc[:, l, 80 + ch:81 + ch], xp[xi][:], ALU.mult, ALU.add,
                                rd=[B_mm[j], B_xp[xi], B_const], wr=[B_xo[oi]])
                            dma("sp", xsl, xo[oi][:], rd=[B_xo[oi]], wr=[B_piece[ch][g]])
                p.barrier()

    with contextlib.ExitStack() as s8:
        xg = [sb(s8, f"fxg{i}", [128, KC, 512], F32) for i in range(2)]
        B_xg = [Buf("fxg0"), Buf("fxg1")]
        sq = sb(s8, "fsq", [128, KC, 512], BF16)
        B_sq = Buf("fsq")
        rstd = [sb(s8, f"frstd{i}", [128, 512], F32) for i in range(2)]
        B_rstd = [Buf("fr0"), Buf("fr1")]
        c0 = HT * L
        for gi in range(CH // 4):
            tok0 = (c0 + gi * 4) * 128
            i = gi % 2
            dma("sp", xg[i][:], xs[:, tok0:tok0 + 512].rearrange("(c p) t -> p c t", p=128), wr=[B_xg[i]])
            tt("pool", sq[:], xg[i][:], xg[i][:], ALU.mult, rd=[B_xg[i]], wr=[B_sq])
            for kc in range(KC):
                mm(st_ps[i][:, :], ones_bf[:], sq[:, kc, :], kc == 0, kc == KC - 1, rd=[B_sq, B_const], wb=B_st[i], first=(kc == 0), sig=(kc == KC - 1))
            act(rstd[i][:], st_ps[i][:, :], AF.Sqrt, rd=[B_st[i], B_const], wr=[B_rstd[i]], scale=1.0 / D, bias=eps_t[:])
            recip(rstd[i][:], rstd[i][:], rd=[B_rstd[i]], wr=[B_rstd[i]])
            for kc in range(KC):
                stt("dve", xg[i][:, kc, :], xg[i][:, kc, :], fg_s[:, kc:kc + 1], rstd[i][:], ALU.mult, ALU.mult,
                    rd=[B_rstd[i], B_const, B_sq], wr=[B_xg[i]] if kc == 0 else (), wr_more=() if kc == 0 else [B_xg[i]])
            dma("sp", yout[:, gi * 512:(gi + 1) * 512].rearrange("(c p) t -> p c t", p=128), xg[i][:], rd=[B_xg[i]])
        p.barrier()

    p.emit(nc)
    es.close()
    return nc


def prep_weights(L, norm1_g, norm2_g, w_ada, b_ada, w_in, out_norm_a, out_norm_b, w_out, na_rpb,
                 w_gate, w_up, w_down, final_g, t5_table):
    f = np.float32
    A = lambda a: np.ascontiguousarray(np.asarray(a, dtype=f))
    w = {}
    w["w_in_r"] = A(np.asarray(w_in)[:L].reshape(L, KC, 128, 12, 512).transpose(0, 3, 2, 1, 4))
    w["w_out_r"] = A(np.asarray(w_out)[:L].reshape(L, KC, 128, 4, 512).transpose(0, 3, 2, 1, 4))
    g_ = np.asarray(w_gate)[:L].reshape(L, KC, 128, FC, 128).transpose(0, 3, 2, 1, 4)
    u_ = np.asarray(w_up)[:L].reshape(L, KC, 128, FC, 128).transpose(0, 3, 2, 1, 4)
    w["w_gu_r"] = A(np.stack([g_, u_], axis=3))
    w["w_dn_r"] = A(np.asarray(w_down)[:L].reshape(L, 2, FH, 128, KC, 128).transpose(0, 1, 4, 3, 2, 5))
    w["w_ada_r"] = A(np.asarray(w_ada)[:L].reshape(L, KC, 128, 12, 8, 128).transpose(0, 3, 2, 4, 1, 5))
    w["b_ada_r"] = A(np.asarray(b_ada)[:L].reshape(L, 96, 128).transpose(0, 2, 1))
    w["n1g_r"] = A(np.asarray(norm1_g)[:L].reshape(L, KC, 128).transpose(0, 2, 1))
    w["n2g_r"] = A(np.asarray(norm2_g)[:L].reshape(L, KC, 128).transpose(0, 2, 1))
    ong = np.concatenate([np.asarray(out_norm_a)[:L], np.asarray(out_norm_b)[:L]], axis=1)
    w["ong_r"] = A(ong.reshape(L, KC, 128).transpose(0, 2, 1))
    w["fg_r"] = A(np.asarray(final_g).reshape(KC, 128).T)
    w["t5"] = A(t5_table)
    rp = np.asarray(na_rpb)[:L]
    w["rpbT"] = A(rp[:, :, ::-1, :].transpose(0, 3, 1, 2).reshape(L, 31, 120))
    w.update(static_tables())
    return w


def run_chunks(L, CH, chunks, weights, debug=False):
    W0 = CH + 2 * HT * L
    halo = HT * L * 128
    nc = build(L, CH, debug)
    in_maps = []
    for ci in range(8):
        m = dict(weights)
        xw = np.zeros((W0 * 128, D), np.float32)
        valid = np.zeros((W0 * 128,), np.float32)
        flags = np.zeros((128, 2), np.float32)
        cvec = np.zeros((D,), np.float32)
        if ci < len(chunks):
            xseq, c, a = chunks[ci]
            T = xseq.shape[0]
            lo, hi = a - halo, a + CH * 128 + halo
            s0, s1 = max(lo, 0), min(hi, T)
            xw[s0 - lo:s1 - lo] = xseq[s0:s1]
            valid[s0 - lo:s1 - lo] = 1.0
            flags[:, 0] = 1.0 if a == 0 else 0.0
            flags[:, 1] = 1.0 if a + CH * 128 == T else 0.0
            cvec = np.asarray(c, np.float32)
        m["xin"] = np.ascontiguousarray(xw.T)
        m["valid"] = np.ascontiguousarray(valid.reshape(W0, 128).T)
        m["flags"] = flags
        m["cvec"] = np.ascontiguousarray(cvec.reshape(KC, 128).T)
        in_maps.append(m)
    res = run_bass_kernel_spmd(nc, in_maps, core_ids=list(range(8)))
    if debug:
        return res.results
    return [np.ascontiguousarray(res.results[ci]["yout"].T) for ci in range(len(chunks))]


def kernel(x_prompt, x_sample, c_prompt, c_sample, t5_table, norm1_g, norm2_g, w_ada, b_ada,
           w_in, out_norm_a, out_norm_b, w_out, na_rpb, w_gate, w_up, w_down, final_g):
    L, CH = 4, 32
    x_prompt = np.asarray(x_prompt, np.float32)
    x_sample = np.asarray(x_sample, np.float32)
    c_prompt = np.asarray(c_prompt, np.float32)
    c_sample = np.asarray(c_sample, np.float32)
    weights = prep_weights(L, norm1_g, norm2_g, w_ada, b_ada, w_in, out_norm_a, out_norm_b, w_out, na_rpb,
                           w_gate, w_up, w_down, final_g, t5_table)
    chunks = [(x_prompt[0], c_prompt[0], i * CH * 128) for i in range(4)]
    chunks += [(x_sample[0], c_sample[0], 0), (x_sample[1], c_sample[1], 0)]
    outs = run_chunks(L, CH, chunks, weights)
    y_prompt = np.concatenate(outs[0:4], axis=0)[None].astype(np.float32)
    y_sample = np.stack([outs[4], outs[5]], axis=0).astype(np.float32)
    return (y_prompt, y_sample)
```
